# Optimizing a Trainium2 kernel written in Bass

```python
import numpy as np
import jax, jax.numpy as jnp
from jax import lax

D_MODEL = 2048
BATCH = 16
SEQ = 2048
DEPTH = 4

CHUNK = 64
PLE_DIM = 256
LRU_WIDTH = 1024
LRU_GROUPS = 16
LRU_CONV = 4
LRU_C = 8.0
GLA_HEADS = 4
GLA_DK = 128
GLA_DV = 256
GLA_RANK = 16
GLA_GATE_NORM = 16.0
GDN_HEADS = 8
GDN_DK = 128
GDN_DV = 128
GDN_CONV = 4
N_BRANCH = 3
D_FF = 5632
FFN_CONV = 3
LN_EPS = 1e-5
NORM_EPS = 1e-6
DEEPNORM_ALPHA = (2 * DEPTH) ** 0.25
DEEPNORM_BETA = (8 * DEPTH) ** -0.25
IN_SPLITS = (LRU_WIDTH, LRU_WIDTH,
             GLA_HEADS * GLA_DK, GLA_HEADS * GLA_DK, GLA_HEADS * GLA_DV, GLA_RANK, GLA_HEADS * GLA_DV,
             GDN_HEADS * GDN_DK, GDN_HEADS * GDN_DK, GDN_HEADS * GDN_DV, GDN_HEADS, GDN_HEADS, GDN_HEADS * GDN_DV)
D_IN = sum(IN_SPLITS)
GDN_QKV = 2 * GDN_HEADS * GDN_DK + GDN_HEADS * GDN_DV

kernel_name = 'hybrid_rglru_gla_gdn_convffn_deepnorm'


def layer_norm(x, g, b):
    xf = x.astype(jnp.float32)
    mu = xf.mean(-1, keepdims=True)
    var = jnp.square(xf - mu).mean(-1, keepdims=True)
    return ((xf - mu) * lax.rsqrt(var + LN_EPS) * g.astype(jnp.float32) + b.astype(jnp.float32)).astype(x.dtype)


def head_rms_norm(x, g):
    return x * lax.rsqrt(jnp.mean(jnp.square(x), -1, keepdims=True) + NORM_EPS) * g.astype(jnp.float32)


def l2_normalize(x):
    return x * lax.rsqrt(jnp.sum(jnp.square(x), -1, keepdims=True) + NORM_EPS)


def causal_depthwise_conv(x, w):
    width = w.shape[0]
    return lax.conv_general_dilated(
        x, w[:, None, :].astype(x.dtype), window_strides=(1,), padding=[(width - 1, 0)],
        dimension_numbers=('NWC', 'WIO', 'NWC'), feature_group_count=x.shape[-1])


def rg_lru(x, w_r, b_r, w_i, b_i, lam):
    B, S, W = x.shape
    xg = x.reshape(B, S, LRU_GROUPS, W // LRU_GROUPS)
    r = jax.nn.sigmoid(jnp.einsum('bsgi,gij->bsgj', xg, w_r.astype(jnp.float32)).reshape(B, S, W) + b_r.astype(jnp.float32))
    i = jax.nn.sigmoid(jnp.einsum('bsgi,gij->bsgj', xg, w_i.astype(jnp.float32)).reshape(B, S, W) + b_i.astype(jnp.float32))
    log_a = -LRU_C * r * jax.nn.softplus(-lam.astype(jnp.float32))
    a = jnp.exp(log_a)
    u = jnp.sqrt(-jnp.expm1(2.0 * log_a)) * (i * x)

    def combine(lhs, rhs):
        a1, b1 = lhs
        a2, b2 = rhs
        return a1 * a2, a2 * b1 + b2

    _, h = lax.associative_scan(combine, (a, u), axis=1)
    return h


def gla_chunked(q, k, v, g):
    B, S, H, K = q.shape
    V = v.shape[-1]
    N = S // CHUNK
    ch = lambda t: t.reshape(B, N, CHUNK, H, -1)
    q = ch(q) * K ** -0.5
    k = ch(k)
    v = ch(v)
    b = jnp.cumsum(ch(g), axis=2)
    b_last = b[:, :, -1]
    q_dec = q * jnp.exp(b)
    k_dec = k * jnp.exp(b_last[:, :, None] - b)
    scores = jnp.einsum('bnthk,bnshk->bnhts', q_dec, k * jnp.exp(-b))
    causal = jnp.tril(jnp.ones((CHUNK, CHUNK), bool))
    o_intra = jnp.einsum('bnhts,bnshv->bnthv', jnp.where(causal, scores, 0.0), v)

    def step(state, inp):
        qd, kd, vc, dl = inp
        o = jnp.einsum('bthk,bhkv->bthv', qd, state)
        state = state * dl[..., None] + jnp.einsum('bshk,bshv->bhkv', kd, vc)
        return state, o

    xs = tuple(jnp.moveaxis(t, 1, 0) for t in (q_dec, k_dec, v, jnp.exp(b_last)))
    _, o_inter = lax.scan(step, jnp.zeros((B, H, K, V), jnp.float32), xs)
    return (o_intra + jnp.moveaxis(o_inter, 0, 1)).reshape(B, S, H, V)


def gated_delta_chunked(q, k, v, g, beta):
    B, S, H, K = q.shape
    V = v.shape[-1]
    N = S // CHUNK
    to_chunks = lambda t: jnp.swapaxes(t.reshape(B, N, CHUNK, H, -1), 2, 3)
    q = to_chunks(q) * K ** -0.5
    k = to_chunks(k)
    v = to_chunks(v)
    gam = jnp.cumsum(to_chunks(g[..., None])[..., 0], axis=-1)
    bt = to_chunks(beta[..., None])
    incl = jnp.tril(jnp.ones((CHUNK, CHUNK), bool))
    strict = jnp.tril(jnp.ones((CHUNK, CHUNK), bool), -1)
    diff = gam[..., :, None] - gam[..., None, :]
    decay_mat = jnp.where(incl, jnp.exp(jnp.where(incl, diff, 0.0)), 0.0)
    kk = jnp.einsum('bnhtk,bnhsk->bnhts', k * bt, k) * decay_mat
    tri = jnp.where(strict, kk, 0.0) + jnp.eye(CHUNK, dtype=kk.dtype)
    rhs = jnp.concatenate([v * bt, k * bt * jnp.exp(gam)[..., None]], axis=-1)
    sol = lax.linalg.triangular_solve(tri, rhs, left_side=True, lower=True, unit_diagonal=True)
    u, w = sol[..., :V], sol[..., V:]
    qk = jnp.einsum('bnhtk,bnhsk->bnhts', q, k) * decay_mat
    q_dec = q * jnp.exp(gam)[..., None]
    k_dec = k * jnp.exp(gam[..., -1:] - gam)[..., None]
    last = jnp.exp(gam[..., -1])

    def step(state, inp):
        qd, kd, uc, wc, a, dl = inp
        v_new = uc - jnp.einsum('bhtk,bhkv->bhtv', wc, state)
        o = jnp.einsum('bhtk,bhkv->bhtv', qd, state) + jnp.einsum('bhts,bhsv->bhtv', a, v_new)
        state = state * dl[..., None, None] + jnp.einsum('bhsk,bhsv->bhkv', kd, v_new)
        return state, o

    xs = tuple(jnp.moveaxis(t, 1, 0) for t in (q_dec, k_dec, u, w, qk, last))
    _, o = lax.scan(step, jnp.zeros((B, H, K, V), jnp.float32), xs)
    return jnp.swapaxes(jnp.moveaxis(o, 0, 1), 2, 3).reshape(B, S, H, V)


def setup_inputs(seed: int = 0) -> dict:
    key = jax.random.key(seed)
    ks = jax.random.split(key, 40)
    f32 = jnp.float32
    nrm = lambda k, shape, scale: jax.random.normal(k, shape, f32) * scale
    gain = lambda k, shape: 1.0 + 0.02 * jax.random.normal(k, shape, f32)
    a0 = jax.random.uniform(ks[9], (DEPTH, LRU_WIDTH), f32, 0.9, 0.999) ** (1.0 / LRU_C)
    dt = jnp.exp(jax.random.uniform(ks[14], (DEPTH, GDN_HEADS), f32, np.log(1e-3), np.log(1e-1)))
    gs = LRU_WIDTH // LRU_GROUPS
    return {
        'x': nrm(ks[0], (BATCH, SEQ, D_MODEL), 1.0),
        'p': nrm(ks[1], (DEPTH, BATCH, SEQ, PLE_DIM), 1.0),
        'w_in': nrm(ks[2], (DEPTH, D_MODEL, D_IN), D_MODEL ** -0.5),
        'lru_conv_w': nrm(ks[3], (DEPTH, LRU_CONV, LRU_WIDTH), LRU_CONV ** -0.5),
        'lru_conv_b': nrm(ks[4], (DEPTH, LRU_WIDTH), 0.02),
        'lru_wr': nrm(ks[5], (DEPTH, LRU_GROUPS, gs, gs), gs ** -0.5),
        'lru_br': nrm(ks[6], (DEPTH, LRU_WIDTH), 0.02),
        'lru_wi': nrm(ks[7], (DEPTH, LRU_GROUPS, gs, gs), gs ** -0.5),
        'lru_bi': nrm(ks[8], (DEPTH, LRU_WIDTH), 0.02),
        'lru_lambda': jnp.log(a0) - jnp.log1p(-a0),
        'gla_wg2': nrm(ks[10], (DEPTH, GLA_RANK, GLA_HEADS * GLA_DK), GLA_RANK ** -0.5),
        'gla_bg2': nrm(ks[11], (DEPTH, GLA_HEADS * GLA_DK), 0.1),
        'gla_norm_g': gain(ks[12], (DEPTH, GLA_DV)),
        'gdn_conv_w': nrm(ks[13], (DEPTH, GDN_CONV, GDN_QKV), GDN_CONV ** -0.5),
        'gdn_a_log': jnp.log(jax.random.uniform(ks[15], (DEPTH, GDN_HEADS), f32, 1.0, 16.0)),
        'gdn_dt_bias': dt + jnp.log(-jnp.expm1(-dt)),
        'gdn_norm_g': gain(ks[16], (DEPTH, GDN_DV)),
        'w_branch': nrm(ks[17], (DEPTH, N_BRANCH, LRU_WIDTH, D_MODEL), LRU_WIDTH ** -0.5),
        'w_merge': nrm(ks[18], (DEPTH, N_BRANCH, D_MODEL, D_MODEL), D_MODEL ** -0.5),
        'b_merge': nrm(ks[19], (DEPTH, N_BRANCH, D_MODEL), 0.02),
        'w_out': nrm(ks[20], (DEPTH, D_MODEL, D_MODEL), D_MODEL ** -0.5 * DEEPNORM_BETA),
        'ln1_g': gain(ks[21], (DEPTH, D_MODEL)),
        'ln1_b': nrm(ks[22], (DEPTH, D_MODEL), 0.02),
        'ffn_w_up': nrm(ks[23], (DEPTH, D_MODEL, D_FF), D_MODEL ** -0.5),
        'ffn_w_gate': nrm(ks[24], (DEPTH, D_MODEL, D_FF), D_MODEL ** -0.5),
        'ffn_conv_w': nrm(ks[25], (DEPTH, FFN_CONV, D_FF), FFN_CONV ** -0.5),
        'ffn_conv_b': nrm(ks[26], (DEPTH, D_FF), 0.02),
        'ffn_w_down': nrm(ks[27], (DEPTH, D_FF, D_MODEL), D_FF ** -0.5 * DEEPNORM_BETA),
        'ple_w_proj': nrm(ks[28], (DEPTH, PLE_DIM, D_MODEL), PLE_DIM ** -0.5 * DEEPNORM_BETA),
        'ple_w_gate': nrm(ks[29], (DEPTH, D_MODEL, D_MODEL), D_MODEL ** -0.5),
        'ln2_g': gain(ks[30], (DEPTH, D_MODEL)),
        'ln2_b': nrm(ks[31], (DEPTH, D_MODEL), 0.02),
    }


def reference(x, p, w_in, lru_conv_w, lru_conv_b, lru_wr, lru_br, lru_wi, lru_bi, lru_lambda,
              gla_wg2, gla_bg2, gla_norm_g, gdn_conv_w, gdn_a_log, gdn_dt_bias, gdn_norm_g,
              w_branch, w_merge, b_merge, w_out, ln1_g, ln1_b,
              ffn_w_up, ffn_w_gate, ffn_conv_w, ffn_conv_b, ffn_w_down,
              ple_w_proj, ple_w_gate, ln2_g, ln2_b):
    B, S, _ = x.shape
    f32 = jnp.float32
    split_at = np.cumsum(IN_SPLITS)[:-1].tolist()
    qk_split = [GDN_HEADS * GDN_DK, 2 * GDN_HEADS * GDN_DK]
    for i in range(DEPTH):
        h = x
        (lru_x, lru_gate, gq, gk, gv, glr, gog,
         dq, dk, dv, da, db, dog) = jnp.split(h @ w_in[i], split_at, axis=-1)

        xa = (causal_depthwise_conv(lru_x, lru_conv_w[i]) + lru_conv_b[i]).astype(f32)
        ya = rg_lru(xa, lru_wr[i], lru_br[i], lru_wi[i], lru_bi[i], lru_lambda[i]) * jax.nn.gelu(lru_gate.astype(f32))

        fg = jax.nn.log_sigmoid((glr @ gla_wg2[i] + gla_bg2[i]).astype(f32)) / GLA_GATE_NORM
        ob = gla_chunked(gq.astype(f32).reshape(B, S, GLA_HEADS, GLA_DK),
                         gk.astype(f32).reshape(B, S, GLA_HEADS, GLA_DK),
                         gv.astype(f32).reshape(B, S, GLA_HEADS, GLA_DV),
                         fg.reshape(B, S, GLA_HEADS, GLA_DK))
        yb = (head_rms_norm(ob, gla_norm_g[i])
              * jax.nn.silu(gog.astype(f32).reshape(B, S, GLA_HEADS, GLA_DV))).reshape(B, S, -1)

        qkv = jax.nn.silu(causal_depthwise_conv(jnp.concatenate([dq, dk, dv], -1), gdn_conv_w[i]).astype(f32))
        cq, ck, cv = jnp.split(qkv, qk_split, axis=-1)
        beta = jax.nn.sigmoid(db.astype(f32))
        decay = -jnp.exp(gdn_a_log[i].astype(f32)) * jax.nn.softplus(da.astype(f32) + gdn_dt_bias[i].astype(f32))
        oc = gated_delta_chunked(l2_normalize(cq.reshape(B, S, GDN_HEADS, GDN_DK)),
                                 l2_normalize(ck.reshape(B, S, GDN_HEADS, GDN_DK)),
                                 cv.reshape(B, S, GDN_HEADS, GDN_DV), decay, beta)
        yc = (head_rms_norm(oc, gdn_norm_g[i])
              * jax.nn.silu(dog.astype(f32).reshape(B, S, GDN_HEADS, GDN_DV))).reshape(B, S, -1)

        branches = (ya, yb, yc)
        merged = jnp.zeros((B, S, D_MODEL), f32)
        for n in range(N_BRANCH):
            gate = jax.nn.sigmoid((h @ w_merge[i, n] + b_merge[i, n]).astype(f32))
            merged = merged + gate * (branches[n].astype(x.dtype) @ w_branch[i, n]).astype(f32)
        mix = merged.astype(x.dtype) @ w_out[i]
        x = layer_norm(DEEPNORM_ALPHA * x + mix, ln1_g[i], ln1_b[i])

        up = x @ ffn_w_up[i]
        gt = causal_depthwise_conv(x @ ffn_w_gate[i], ffn_conv_w[i]) + ffn_conv_b[i]
        ffn = (jax.nn.silu(gt) * up) @ ffn_w_down[i]
        ple = (p[i] @ ple_w_proj[i]) * jax.nn.sigmoid(x @ ple_w_gate[i])
        x = layer_norm(DEEPNORM_ALPHA * x + ffn + ple, ln2_g[i], ln2_b[i])
    return x
```

```python
import contextlib
import numpy as np
import concourse.bass as bass
import concourse.mybir as mybir
from concourse.bass_utils import run_bass_kernel_spmd

F32 = mybir.dt.float32
BF16 = mybir.dt.bfloat16
AF = mybir.ActivationFunctionType
ALU = mybir.AluOpType

D = 2048
T = 2048
NKT = 16
NTB = 4
TB = 512
DEPTH = 4
DFF = 5632
NFF = 44
ALPHA = (2 * DEPTH) ** 0.25
LN_EPS = 1e-5
NORM_EPS = 1e-6
N_CORES = 8
SEQ_PER_CORE = 2

IN_COLS = dict(lru_x=0, lru_gate=1024, gq=2048, gk=2560, gv=3072, glr=4096, gog=4112,
               dq=5136, dk=6160, dv=7184, da=8208, db=8216, dog=8224)
MT_LRUX, MT_LRUG, MT_GQ, MT_GK, MT_GV, MT_GOG = 0, 8, 16, 20, 24, 32
MT_DQ, MT_DK, MT_DV, MT_DOG, MT_GLR, MT_DAB, MT_DB = 40, 48, 56, 64, 72, 73, 74
N_IN_MT = 75

VC = {}
_o = 0
for _n, _w in [("lru_cw", 32), ("lru_cb", 8), ("lru_br", 8), ("lru_bi", 8), ("lru_lam", 8), ("gla_bg2", 4),
               ("gla_ng", 2), ("gdn_cw", 96), ("gdn_ng", 1), ("b_merge", 48), ("ln1_g", 16), ("ln1_b", 16),
               ("ln2_g", 16), ("ln2_b", 16), ("ffn_cw", 132), ("ffn_cb", 44), ("gdn_alog", 1), ("gdn_dtb", 1)]:
    VC[_n] = _o
    _o += _w
NV = _o

C_ID, C_MIT, C_MST, C_MS, C_CM = 0, 128, 256, 384, 512
NCON = 512 + T


def make_consts():
    c = np.zeros((128, NCON), np.float32)
    i = np.arange(128)
    same = (i[:, None] // 64) == (i[None, :] // 64)
    c[:, C_ID:C_ID + 128] = np.eye(128)
    c[:, C_MIT:C_MIT + 128] = same & (i[:, None] <= i[None, :])
    c[:, C_MST:C_MST + 128] = same & (i[:, None] < i[None, :])
    c[:, C_MS:C_MS + 128] = same & (i[:, None] > i[None, :])
    cm = np.ones(T, np.float32)
    cm[::64] = 0.0
    c[:, C_CM:] = cm[None, :]
    return c


def panelize(w, nk):
    K, M = w.shape
    return np.ascontiguousarray(w.reshape(nk, 128, M // 128, 128).transpose(2, 1, 0, 3))


def prep_weights(inp):
    L = DEPTH
    out = {}
    w_in = inp["w_in"]
    win_p = np.zeros((L, N_IN_MT, 128, NKT, 128), np.float32)
    for l in range(L):
        w = w_in[l]
        order = [("lru_x", 8), ("lru_gate", 8), ("gq", 4), ("gk", 4), ("gv", 8), ("gog", 8),
                 ("dq", 8), ("dk", 8), ("dv", 8), ("dog", 8)]
        mt = 0
        for name, n in order:
            c0 = IN_COLS[name]
            win_p[l, mt:mt + n] = panelize(w[:, c0:c0 + n * 128], NKT)
            mt += n
        wk = w.reshape(NKT, 128, -1)
        win_p[l, MT_GLR, :, :, 0:16] = wk[:, :, 4096:4112].transpose(1, 0, 2)
        win_p[l, MT_DAB, :, :, 0:8] = wk[:, :, 8208:8216].transpose(1, 0, 2)
        win_p[l, MT_DB, :, :, 0:8] = wk[:, :, 8216:8224].transpose(1, 0, 2)
    out["win_p"] = win_p
    out["wm_p"] = np.stack([np.stack([panelize(inp["w_merge"][l, n], 16) for n in range(3)]) for l in range(L)])
    out["wb_p"] = np.stack([np.stack([panelize(inp["w_branch"][l, n], 8) for n in range(3)]) for l in range(L)])
    out["wo_p"] = np.stack([panelize(inp["w_out"][l], 16) for l in range(L)])
    out["wup_p"] = np.stack([panelize(inp["ffn_w_up"][l], 16) for l in range(L)])
    out["wgt_p"] = np.stack([panelize(inp["ffn_w_gate"][l], 16) for l in range(L)])
    out["wdn_p"] = np.stack([panelize(inp["ffn_w_down"][l], NFF) for l in range(L)])
    out["wpg_p"] = np.stack([panelize(inp["ple_w_gate"][l], 16) for l in range(L)])
    out["wpp_p"] = np.stack([panelize(inp["ple_w_proj"][l], 2) for l in range(L)])
    bd = np.zeros((L, 2, 8, 128, 128), np.float32)
    for l in range(L):
        for gi, nm in enumerate(("lru_wr", "lru_wi")):
            w = inp[nm][l]
            for g in range(16):
                j, o = g // 2, (g % 2) * 64
                bd[l, gi, j, o:o + 64, o:o + 64] = w[g]
    out["bd"] = bd
    out["wg2"] = np.ascontiguousarray(inp["gla_wg2"])
    v = np.zeros((L, 128, NV), np.float32)

    def put(name, arr, nft):
        v[:, :, VC[name]:VC[name] + nft] = arr.reshape(L, nft, 128).transpose(0, 2, 1)

    v[:, :, VC["lru_cw"]:VC["lru_cw"] + 32] = inp["lru_conv_w"].reshape(L, 4, 8, 128).transpose(0, 3, 2, 1).reshape(L, 128, 32)
    put("lru_cb", inp["lru_conv_b"], 8)
    put("lru_br", inp["lru_br"], 8)
    put("lru_bi", inp["lru_bi"], 8)
    put("lru_lam", inp["lru_lambda"], 8)
    put("gla_bg2", inp["gla_bg2"], 4)
    put("gla_ng", inp["gla_norm_g"], 2)
    v[:, :, VC["gdn_cw"]:VC["gdn_cw"] + 96] = inp["gdn_conv_w"].reshape(L, 4, 24, 128).transpose(0, 3, 2, 1).reshape(L, 128, 96)
    put("gdn_ng", inp["gdn_norm_g"], 1)
    put("b_merge", inp["b_merge"].reshape(L, 3 * D), 48)
    put("ln1_g", inp["ln1_g"], 16)
    put("ln1_b", inp["ln1_b"], 16)
    put("ln2_g", inp["ln2_g"], 16)
    put("ln2_b", inp["ln2_b"], 16)
    v[:, :, VC["ffn_cw"]:VC["ffn_cw"] + 132] = inp["ffn_conv_w"].reshape(L, 3, NFF, 128).transpose(0, 3, 2, 1).reshape(L, 128, 132)
    put("ffn_cb", inp["ffn_conv_b"], NFF)
    v[:, 0:8, VC["gdn_alog"]] = inp["gdn_a_log"]
    v[:, 0:8, VC["gdn_dtb"]] = inp["gdn_dt_bias"]
    out["vecs"] = v
    out["consts"] = make_consts()
    return out


class Eng:
    def __init__(self, name, eng, sem):
        self.name, self.eng, self.sem = name, eng, sem
        self.count = 0
        self.waited = {}


class DmaQ:
    def __init__(self, name, eng, sems, host):
        self.name, self.eng, self.sems, self.host = name, eng, sems, host
        self.totals = [0] * len(sems)
        self.idx = 0
        self.waited = {}


class KB:
    def __init__(self, nc, n_dma_sems=24):
        self.nc = nc
        self.es = contextlib.ExitStack()
        self.last_write = {}
        self.readers = {}
        self.n_inst = 0
        self.n_wait = 0
        mk = lambda n: self.es.enter_context(nc.semaphore(n))
        self.pe = Eng("pe", nc.tensor, mk("s_pe"))
        self.act = Eng("act", nc.scalar, mk("s_act"))
        self.dve = Eng("dve", nc.vector, mk("s_dve"))
        self.pool = Eng("pool", nc.gpsimd, mk("s_pool"))
        self.q_sp = DmaQ("q_sp", nc.sync, [mk(f"s_qsp{i}") for i in range(n_dma_sems)], None)
        self.q_pool = DmaQ("q_pool", nc.gpsimd, [mk(f"s_qpl{i}") for i in range(n_dma_sems)], self.pool)

    def sbuf(self, name, shape, dtype):
        return self.es.enter_context(self.nc.sbuf_tensor(name, list(shape), dtype))

    def psum(self, name, shape, dtype=F32):
        return self.es.enter_context(self.nc.psum_tensor(name, list(shape), dtype))

    def close(self):
        self.es.close()

    def _deps(self, reads, writes):
        deps = {}

        def add(d):
            for s, v in d.items():
                if deps.get(s, (None, 0))[1] < v[1]:
                    deps[s] = v

        for k in reads:
            add(self.last_write.get(k, {}))
        for k in writes:
            add(self.last_write.get(k, {}))
            add(self.readers.get(k, {}))
        return deps

    def _emit_waits(self, waiter, eng, deps):
        for sid, (sem, val) in deps.items():
            if waiter.waited.get(sid, 0) >= val:
                continue
            eng.wait_ge(sem, val)
            self.n_wait += 1
            waiter.waited[sid] = val

    def _record(self, ev, reads, writes):
        sid, sem, val = ev
        for k in reads:
            r = self.readers.setdefault(k, {})
            if r.get(sid, (None, 0))[1] < val:
                r[sid] = (sem, val)
        for k in writes:
            self.last_write[k] = {sid: (sem, val)}
            self.readers[k] = {}

    def op(self, E, fn, reads=(), writes=()):
        deps = self._deps(reads, writes)
        if E is self.pe:
            deps.pop(id(E.sem), None)
        self._emit_waits(E, E.eng, deps)
        inst = fn(E.eng)
        E.count += 1
        inst.then_inc(E.sem, 1)
        self.n_inst += 1
        self._record((id(E.sem), E.sem, E.count), reads, writes)
        return inst

    def dma(self, Q, out, in_, reads=(), writes=()):
        deps = self._deps(reads, writes)
        waiter = Q.host if Q.host is not None else Q
        i = Q.idx
        Q.idx = (Q.idx + 1) % len(Q.sems)
        sem = Q.sems[i]
        if Q.totals[i] > 0:
            deps[id(sem)] = (sem, Q.totals[i])
        self._emit_waits(waiter, Q.eng, deps)
        inst = Q.eng.dma_start(out=out, in_=in_)
        Q.totals[i] += 16
        inst.then_inc(sem, 16)
        self.n_inst += 1
        self._record((id(sem), sem, Q.totals[i]), reads, writes)
        return inst

    def wait_all(self, E, keys):
        self._emit_waits(E, E.eng, self._deps(keys, ()))


class WStream:
    def __init__(self, P, panels, nk, slots, name, depth=None):
        self.P, self.panels, self.nk, self.slots, self.name = P, panels, nk, slots, name
        self.depth = depth or len(slots)
        self.nxt = 0

    def get(self, i):
        kb = self.P.kb
        while self.nxt < min(len(self.panels), i + self.depth):
            j = self.nxt
            s = j % len(self.slots)
            kb.dma(kb.q_pool, self.slots[s][:, 0:self.nk, :], self.panels[j], writes=[(self.name, s)])
            self.nxt += 1
        s = i % len(self.slots)
        return self.slots[s], (self.name, s)


class Prog:
    def __init__(self, n_seq, layers, dbg=(), lite=False):
        nc = self.nc = bass.Bass("TRN2", target_bir_lowering=False)
        kb = self.kb = KB(nc)
        self.n_seq, self.layers, self.dbg = n_seq, layers, dbg

        def dt(name, shape, dtype=F32, kind="ExternalInput"):
            if lite and kind == "ExternalInput" and name in ("win_p", "bd", "wg2", "vecs", "wm_p", "wb_p", "wo_p", "wup_p", "wgt_p", "wdn_p", "wpg_p", "wpp_p", "pT"):
                shape = [1] + list(shape[1:])
            return nc.dram_tensor(name, list(shape), dtype, kind=kind).ap()

        self.xT = dt("xT", [n_seq, D, T])
        self.pT = dt("pT", [DEPTH, n_seq, 256, T])
        self.win = dt("win_p", [DEPTH, N_IN_MT, 128, NKT, 128])
        self.wm = dt("wm_p", [DEPTH, 3, 16, 128, 16, 128])
        self.wb = dt("wb_p", [DEPTH, 3, 16, 128, 8, 128])
        self.wo = dt("wo_p", [DEPTH, 16, 128, 16, 128])
        self.wup = dt("wup_p", [DEPTH, NFF, 128, 16, 128])
        self.wgt = dt("wgt_p", [DEPTH, NFF, 128, 16, 128])
        self.wdn = dt("wdn_p", [DEPTH, 16, 128, NFF, 128])
        self.wpg = dt("wpg_p", [DEPTH, 16, 128, 16, 128])
        self.wpp = dt("wpp_p", [DEPTH, 16, 128, 2, 128])
        self.bd = dt("bd", [DEPTH, 2, 8, 128, 128])
        self.wg2 = dt("wg2", [DEPTH, 16, 512])
        self.vecs = dt("vecs", [DEPTH, 128, NV])
        self.consts = dt("consts", [128, NCON])
        self.outT = dt("outT", [n_seq, D, T], kind="ExternalOutput")
        self.dbg_out = {n: dt("dbg_" + n, shp, dty, kind="ExternalOutput") for n, shp, dty in dbg}
        self.yscr = dt("yscr", [3, 1024, T], BF16, "Internal")
        self.gscr = dt("gscr", [3, D, T], BF16, "Internal")
        self.zscr = dt("zscr", [D, T], F32, "Internal")
        self.xres = dt("xres", [D, T], F32, "Internal")
        self.ascr = dt("ascr", [DFF, T], BF16, "Internal")
        if any(n == "merged" for n, _, _ in dbg):
            self.yin = dt("yin", [3, 1024, T], BF16)
        if any(n == "act" for n, _, _ in dbg):
            self.x1in = dt("x1in", [D, T])
        self.con = kb.sbuf("con", [128, NCON], F32)
        self.identb = kb.sbuf("identb", [128, 128], BF16)
        self.onesb = kb.sbuf("onesb", [128, 128], BF16)
        self.onesf = kb.sbuf("onesf", [128, 128], F32)
        self.vec = kb.sbuf("vec", [128, NV], F32)
        self.der = kb.sbuf("der", [128, 32], F32)
        self.XB = kb.sbuf("XB", [128, NKT, T], BF16)
        self.ws = [kb.sbuf(f"ws{i}", [128, NKT, 128], BF16) for i in range(3)]
        self.pg = [kb.psum(f"pg{i}", [128, TB]) for i in range(2)]
        self.pm = [kb.psum(f"pm{i}", [128, TB]) for i in range(5)]
        self.pt = kb.psum("pt", [128, 2 * TB], BF16)
        self.pg_i = 0
        self.out_keys = []

    def vcol(self, name, i=0, p=128):
        c = VC[name] + i
        return self.vec[0:p, c:c + 1]

    def next_pg(self):
        i = self.pg_i
        self.pg_i = (i + 1) % len(self.pg)
        return self.pg[i], ("pg", i)

    def A(self, out, in_, func, reads, writes, bias=None, scale=None):
        kw = {}
        if bias is not None:
            kw["bias"] = bias
        if scale is not None:
            kw["scale"] = scale
        return self.kb.op(self.kb.act, lambda e: e.activation(out=out, in_=in_, func=func, **kw), reads, writes)

    def TT(self, out, a, b, op, reads, writes, E=None):
        return self.kb.op(E or self.kb.dve, lambda e: e.tensor_tensor(out=out, in0=a, in1=b, op=op), reads, writes)

    def TS(self, out, a, s1, op0, reads, writes, s2=None, op1=None, E=None):
        if op1 is None:
            return self.kb.op(E or self.kb.dve, lambda e: e.tensor_scalar(out=out, in0=a, scalar1=s1, scalar2=None, op0=op0), reads, writes)
        return self.kb.op(E or self.kb.dve, lambda e: e.tensor_scalar(out=out, in0=a, scalar1=s1, scalar2=s2, op0=op0, op1=op1), reads, writes)

    def STT(self, out, a, s, b, op0, op1, reads, writes):
        return self.kb.op(self.kb.dve, lambda e: e.scalar_tensor_tensor(out=out, in0=a, scalar=s, in1=b, op0=op0, op1=op1), reads, writes)

    def MM(self, out, lhsT, rhs, start, stop, reads, writes):
        return self.kb.op(self.kb.pe, lambda e: e.matmul(out, lhsT=lhsT, rhs=rhs, start=start, stop=stop), reads, writes)

    def CP(self, out, in_, reads, writes, E=None):
        return self.kb.op(E or self.kb.dve, lambda e: e.tensor_copy(out=out, in_=in_), reads, writes)

    def barrier(self):
        kb = self.kb
        engs = [kb.pe, kb.act, kb.dve, kb.pool]
        for E in engs + [kb.q_sp]:
            eng = E.eng
            deps = {}
            for E2 in engs:
                if E2 is not E and E2.count > 0:
                    deps[id(E2.sem)] = (E2.sem, E2.count)
            for Q in (kb.q_sp, kb.q_pool):
                for sem, tot in zip(Q.sems, Q.totals):
                    if tot > 0:
                        deps[id(sem)] = (sem, tot)
            kb._emit_waits(E, eng, deps)

    def dump(self, name, src_ap, dst_slice, reads):
        if name in self.dbg_out:
            self.kb.dma(self.kb.q_sp, dst_slice(self.dbg_out[name]), src_ap, reads=reads, writes=[("dbg", name)])
            self.out_keys.append(("dbg", name))

    def setup(self):
        kb = self.kb
        kb.dma(kb.q_sp, self.con[:], self.consts, writes=["con"])
        self.CP(self.identb[:], self.con[:, C_ID:C_ID + 128], ["con"], ["identb"])
        kb.op(kb.dve, lambda e: e.memset(self.onesb[:], 1.0), (), ["onesb"])
        kb.op(kb.dve, lambda e: e.memset(self.onesf[:], 1.0), (), ["onesf"])

    def layer_setup(self, l):
        kb = self.kb
        kb.dma(kb.q_sp, self.vec[:], self.vecs[l], writes=["vec"])
        d = self.der
        lam = self.vec[:, VC["lru_lam"]:VC["lru_lam"] + 8]
        self.A(d[:, 0:8], lam, AF.Exp, ["vec"], ["der"], scale=-1.0)
        self.A(d[:, 0:8], d[:, 0:8], AF.Ln, ["der"], ["der"], bias=1.0)
        self.TS(d[:, 8:16], d[:, 0:8], -16.0, ALU.mult, ["der"], ["der"])
        self.TS(d[:, 0:8], d[:, 0:8], -8.0, ALU.mult, ["der"], ["der"])
        self.TS(d[:, 16:20], self.vec[:, VC["gla_bg2"]:VC["gla_bg2"] + 4], -1.0, ALU.mult, ["vec", "der"], ["der"])
        self.A(d[0:8, 20:21], self.vcol("gdn_alog", 0, 8), AF.Exp, ["vec", "der"], ["der"])
        self.TS(d[0:8, 20:21], d[0:8, 20:21], -1.0, ALU.mult, ["der"], ["der"])

    def load_x0(self, s, src=None):
        kb = self.kb
        src = self.xT[s] if src is None else src
        for ft in range(NKT):
            kb.dma(kb.q_pool, self.XB[:, ft, :], src[ft * 128:(ft + 1) * 128, :], writes=[("XB", ft)])

    def xb_keys(self, k):
        return [("XB", k)]

    def proj_xb(self, wslot, wkey, M, epi, nk=NKT):
        for tb in range(NTB):
            ps, pkey = self.next_pg()
            for k in range(nk):
                self.MM(ps[0:M, :], wslot[:, k, 0:M], self.XB[:, k, tb * TB:(tb + 1) * TB], k == 0, k == nk - 1,
                        [wkey, ("XB", k)], [pkey])
            epi(tb, ps, pkey)

    def phase_alloc(self):
        es = contextlib.ExitStack()
        nc = self.nc
        self._uid = getattr(self, "_uid", 0) + 1
        u = self._uid

        def sb(name, shape, dtype=F32):
            return es.enter_context(nc.sbuf_tensor(f"{name}_{u}", list(shape), dtype))

        return es, sb

    def phase_lru(self, l):
        kb = self.kb
        self.barrier()
        es, sb = self.phase_alloc()
        with es:
            lxp = [sb(f"lxp{i}", [128, T + 4]) for i in range(2)]
            gt = [sb(f"gt{i}", [128, T]) for i in range(2)]
            xa, r, ii, a, hh = [sb(n, [128, T]) for n in ("xa", "r", "ii", "a", "hh")]
            xab = sb("xab", [128, T], BF16)
            yt = [sb(f"yt{i}", [128, T], BF16) for i in range(2)]
            bdw = [sb(f"bdw{i}", [128, 2, 128], BF16) for i in range(2)]
            for i in range(2):
                kb.op(kb.dve, lambda e: e.memset(lxp[i][:, 0:4], 0.0), (), [("lxp", i, "h")])
            panels = []
            for j in range(8):
                panels += [self.win[l, MT_LRUX + j], self.win[l, MT_LRUG + j]]
            wst = WStream(self, panels, NKT, self.ws, "ws")
            for j in range(8):
                s = j % 2
                kb.dma(kb.q_pool, bdw[s][:, :, :], self.bd[l, :, j].rearrange("g p m -> p g m"), writes=[("bdw", s)])
                w0, k0 = wst.get(2 * j)

                def epi_x(tb, ps, pkey):
                    self.A(lxp[s][:, 4 + tb * TB:4 + (tb + 1) * TB], ps[:], AF.Copy, [pkey], [("lxp", s, tb)])
                self.proj_xb(w0, k0, 128, epi_x)
                self.bgs(24)
                w1, k1 = wst.get(2 * j + 1)

                def epi_g(tb, ps, pkey):
                    self.A(gt[s][:, tb * TB:(tb + 1) * TB], ps[:], AF.Copy, [pkey], [("gt", s, tb)])
                self.proj_xb(w1, k1, 128, epi_g)
                self.bgs(24)
                lk = [("lxp", s, tb) for tb in range(NTB)] + [("lxp", s, "h")]
                gk = [("gt", s, tb) for tb in range(NTB)]
                cw = lambda tap: self.vcol("lru_cw", j * 4 + tap)
                self.TS(xa[:], lxp[s][:, 4:4 + T], cw(3), ALU.mult, lk + ["vec"], ["xa"], s2=self.vcol("lru_cb", j), op1=ALU.add)
                for tap in (2, 1, 0):
                    self.STT(xa[:], lxp[s][:, 1 + tap:1 + tap + T], cw(tap), xa[:], ALU.mult, ALU.add, lk + ["xa", "vec"], ["xa"])
                self.CP(xab[:], xa[:], ["xa"], ["xab"], E=kb.pool)
                for gi, (dst, bname) in enumerate(((r, "lru_br"), (ii, "lru_bi"))):
                    for tb in range(NTB):
                        ps, pkey = self.next_pg()
                        self.MM(ps[:], bdw[s][:, gi, :], xab[:, tb * TB:(tb + 1) * TB], True, True, [("bdw", s), "xab"], [pkey])
                        self.A(dst[:, tb * TB:(tb + 1) * TB], ps[:], AF.Sigmoid, [pkey, "vec"], [(bname, tb)], bias=self.vcol(bname, j))
                    self.bgs(24)
                rk = [("lru_br", tb) for tb in range(NTB)]
                ik = [("lru_bi", tb) for tb in range(NTB)]
                self.A(a[:], r[:], AF.Exp, rk + ["der"], ["a"], scale=self.der[:, j:j + 1])
                self.A(r[:], r[:], AF.Exp, rk + ["der"], rk, scale=self.der[:, 8 + j:9 + j])
                self.TS(r[:], r[:], -1.0, ALU.mult, rk, rk, s2=1.0, op1=ALU.add)
                self.A(r[:], r[:], AF.Sqrt, rk, rk)
                self.TT(ii[:], ii[:], xa[:], ALU.mult, ik + ["xa"], ik)
                self.TT(ii[:], ii[:], r[:], ALU.mult, ik + rk, ik)
                kb.op(kb.dve, lambda e: e.tensor_tensor_scan(out=hh[:], data0=a[:], data1=ii[:], initial=0.0, op0=ALU.mult, op1=ALU.add),
                      ["a"] + ik, ["hh"])
                self.A(xa[:], gt[s][:], AF.Square, gk + ["xa"], ["xa"])
                self.TS(xa[:], xa[:], 0.044715, ALU.mult, ["xa"], ["xa"], s2=1.0, op1=ALU.add)
                self.TT(xa[:], xa[:], gt[s][:], ALU.mult, ["xa"] + gk, ["xa"])
                self.A(xa[:], xa[:], AF.Sigmoid, ["xa"], ["xa"], scale=1.5957691216057308)
                self.TT(xa[:], xa[:], gt[s][:], ALU.mult, ["xa"] + gk, ["xa"])
                self.TT(yt[s][:], hh[:], xa[:], ALU.mult, ["hh", "xa"], [("yt", s)])
                kb.dma(kb.q_sp, self.yscr[0, j * 128:(j + 1) * 128, :], yt[s][:], reads=[("yt", s)], writes=[("yscr", 0, j)])
                self.dump("ya", yt[s][:], lambda o: o[j * 128:(j + 1) * 128, :], [("yt", s)])
            self.bg_close_group()
            self.barrier()

    def PT(self, out, in_, reads, writes):
        return self.kb.op(self.kb.pe, lambda e: e.transpose(out=out, in_=in_, identity=self.identb[:]), reads + ["identb"], writes)

    def phase_gla(self, l):
        kb = self.kb
        self.barrier()
        es, sb = self.phase_alloc()
        with es:
            glr16 = sb("glr16", [16, T], BF16)
            wg2b = sb("wg2b", [16, 512], BF16)
            b, eb, enb = [sb(n, [128, T]) for n in ("b", "eb", "enb")]
            ebl = sb("ebl", [128, 32])
            qd, ki, kd = [sb(n, [128, T], BF16) for n in ("qd", "ki", "kd")]
            vtm = sb("vtm", [128, 16, 256], BF16)
            sg = sb("sg", [128, 2, T], BF16)
            O = sb("O", [128, 2, T])
            S = sb("S", [128, 256])
            Sb = [sb(f"Sb{i}", [128, 256], BF16) for i in range(3)]
            scT = [sb(f"scT{i}", [128, 128], BF16) for i in range(2)]
            kdT = [sb(f"kdT{i}", [128, 128], BF16) for i in range(2)]
            wv = sb("wv", [128, NKT, 256], BF16)
            sq = sb("sq", [128, 2, TB], BF16)
            rs = sb("rs", [128, TB])
            t2 = sb("t2", [128, TB])
            yt = [sb(f"yt{i}", [128, T], BF16) for i in range(2)]
            pmA, pmS, pmO = self.pm[0], self.pm[1], self.pm[2]
            kb.dma(kb.q_pool, wg2b[:], self.wg2[l], writes=["wg2b"])
            panels = [self.win[l, MT_GLR]]
            for h in range(4):
                panels += [self.win[l, MT_GQ + h], self.win[l, MT_GK + h], self.win[l, MT_GOG + 2 * h], self.win[l, MT_GOG + 2 * h + 1]]
            wst = WStream(self, panels, NKT, self.ws, "ws")
            w0, k0 = wst.get(0)

            def epi_glr(tb, ps, pkey):
                self.A(glr16[:, tb * TB:(tb + 1) * TB], ps[0:16, :], AF.Copy, [pkey], [("glr", tb)])
            self.proj_xb(w0, k0, 16, epi_glr)
            yi = 0
            for h in range(4):
                tbs = lambda tb: slice(tb * TB, (tb + 1) * TB)
                for tb in range(NTB):
                    ps, pkey = self.next_pg()
                    self.MM(ps[:], wg2b[:, h * 128:(h + 1) * 128], glr16[:, tbs(tb)], True, True, ["wg2b", ("glr", tb)], [pkey])
                    self.A(enb[:, tbs(tb)], ps[:], AF.Exp, [pkey, "der"], [("enb", tb)], scale=-1.0, bias=self.der[:, 16 + h:17 + h])
                    self.A(enb[:, tbs(tb)], enb[:, tbs(tb)], AF.Ln, [("enb", tb)], [("enb", tb)], bias=1.0)
                    self.TS(enb[:, tbs(tb)], enb[:, tbs(tb)], -1.0 / 16.0, ALU.mult, [("enb", tb)], [("enb", tb)])
                ek = [("enb", tb) for tb in range(NTB)]
                kb.op(kb.dve, lambda e: e.tensor_tensor_scan(out=b[:], data0=self.con[:, C_CM:C_CM + T], data1=enb[:], initial=0.0,
                                                             op0=ALU.mult, op1=ALU.add), ek + ["con"], ["b"])
                self.A(eb[:], b[:], AF.Exp, ["b"], ["eb"])
                self.A(enb[:], b[:], AF.Exp, ["b"] + ek, ek, scale=-1.0)
                self.A(ebl[:], b[:, 63:T:64], AF.Exp, ["b"], ["ebl"])
                wq, kq = wst.get(1 + 4 * h)

                def epi_q(tb, ps, pkey):
                    self.STT(qd[:, tbs(tb)], ps[:], 128 ** -0.5, eb[:, tbs(tb)], ALU.mult, ALU.mult, [pkey, "eb"], [("qd", tb)])
                self.proj_xb(wq, kq, 128, epi_q)
                self.bgs(32)
                wk, kk = wst.get(2 + 4 * h)

                def epi_k(tb, ps, pkey):
                    self.TT(ki[:, tbs(tb)], ps[:], enb[:, tbs(tb)], ALU.mult, [pkey] + ek, [("ki", tb)])
                    self.TT(kd[:, tbs(tb)].rearrange("p (c t) -> p c t", t=64), ki[:, tbs(tb)].rearrange("p (c t) -> p c t", t=64),
                            ebl[:, tb * 8:(tb + 1) * 8].unsqueeze(2).to_broadcast([128, 8, 64]), ALU.mult,
                            [("ki", tb), "ebl"], [("kd", tb)], E=kb.pool)
                self.proj_xb(wk, kk, 128, epi_k)
                self.bgs(32)
                for half in range(2):
                    kb.dma(kb.q_pool, wv[:, :, half * 128:(half + 1) * 128], self.win[l, MT_GV + 2 * h + half], writes=[("wv", half)])
                for tt in range(16):
                    ps, pkey = self.next_pg()
                    for k in range(NKT):
                        self.MM(ps[:, 0:256], self.XB[:, k, tt * 128:(tt + 1) * 128], wv[:, k, :], k == 0, k == NKT - 1,
                                [("wv", 0), ("wv", 1), ("XB", k)], [pkey])
                    self.A(vtm[:, tt, :], ps[:, 0:256], AF.Copy, [pkey], [("vtm", tt)])
                    self.bgs(4)
                for vh in range(2):
                    wg, kg = wst.get(3 + 4 * h + vh)

                    def epi_g(tb, ps, pkey):
                        self.A(sg[:, vh, tbs(tb)], ps[:], AF.Silu, [pkey], [("sg", vh, tb)])
                    self.proj_xb(wg, kg, 128, epi_g)
                kb.op(kb.dve, lambda e: e.memset(S[:], 0.0), (), ["S"])
                kb.op(kb.dve, lambda e: e.memset(Sb[0][:], 0.0), (), [("Sb", 0)])
                si = 0
                for pr in range(16):
                    tb = pr // 4
                    cols = slice(pr * 128, (pr + 1) * 128)
                    x = pr % 2
                    self.MM(pmA[:, 0:128], ki[:, cols], qd[:, cols], True, True, [("ki", tb), ("qd", tb)], ["pmA"])
                    self.TT(scT[x][:], pmA[:, 0:128], self.con[:, C_MIT:C_MIT + 128], ALU.mult, ["pmA", "con"], [("scT", x)])
                    self.PT(self.pt[:, 0:128], kd[:, cols], [("kd", tb)], ["pt"])
                    self.A(kdT[x][:], self.pt[:, 0:128], AF.Copy, ["pt"], [("kdT", x)])
                    self.bgs(8)
                    sbi = [si]
                    for c in range(2):
                        pp = slice(c * 64, (c + 1) * 64)
                        self.MM(pmS[:, 0:256], kdT[x][pp, :], vtm[pp, pr, :], True, True, [("kdT", x), ("vtm", pr)], ["pmS"])
                        ci = pr * 2 + c
                        self.STT(S[:], S[:], ebl[:, ci:ci + 1], pmS[:, 0:256], ALU.mult, ALU.add, ["S", "ebl", "pmS"], ["S"])
                        si = (si + 1) % 3
                        self.A(Sb[si][:], S[:], AF.Copy, ["S"], [("Sb", si)])
                        sbi.append(si)
                        self.bgs(8)
                        if c == 0:
                            for vh in range(2):
                                vs = slice(vh * 128, (vh + 1) * 128)
                                self.MM(pmO[:, vs], vtm[:, pr, vs], scT[x][:], True, False, [("vtm", pr), ("scT", x)], ["pmO"])
                                for cc in range(2):
                                    self.MM(pmO[:, vh * 128 + cc * 64:vh * 128 + (cc + 1) * 64], Sb[sbi[cc]][:, vs],
                                            qd[:, pr * 128 + cc * 64:pr * 128 + (cc + 1) * 64], False, cc == 1,
                                            [("Sb", sbi[cc]), ("qd", tb)], ["pmO"])
                            self.A(O[:, :, cols], pmO[:, 0:256].rearrange("p (a b) -> p a b", a=2), AF.Copy, ["pmO"], [("O", pr)])
                for tb in range(NTB):
                    ok = [("O", pr) for pr in range(tb * 4, tb * 4 + 4)]
                    self.A(sq[:], O[:, :, tbs(tb)], AF.Square, ok, ["sq"])
                    ps, pkey = self.next_pg()
                    for vh in range(2):
                        self.MM(ps[:], self.onesb[:], sq[:, vh, :], vh == 0, vh == 1, ["onesb", "sq"], [pkey])
                    self.A(rs[:], ps[:], AF.Ln, [pkey], ["rs"], scale=1.0 / 256.0, bias=NORM_EPS)
                    self.A(rs[:], rs[:], AF.Exp, ["rs"], ["rs"], scale=-0.5)
                    for vh in range(2):
                        self.STT(t2[:], O[:, vh, tbs(tb)], self.vcol("gla_ng", vh), rs[:], ALU.mult, ALU.mult, ok + ["vec", "rs"], ["t2"])
                        self.TT(yt[vh][:, tbs(tb)], t2[:], sg[:, vh, tbs(tb)], ALU.mult, ["t2", ("sg", vh, tb)], [("ytb", vh, tb)])
                for vh in range(2):
                    f = h * 2 + vh
                    yk = [("ytb", vh, tb) for tb in range(NTB)]
                    kb.dma(kb.q_sp, self.yscr[1, f * 128:(f + 1) * 128, :], yt[vh][:], reads=yk, writes=[("yscr", 1, f)])
                    self.dump("yb", yt[vh][:], lambda o: o[f * 128:(f + 1) * 128, :], yk)
                    self.dump("ob", O[:, vh, :], lambda o: o[f * 128:(f + 1) * 128, :], [("O", pr) for pr in range(16)])
            self.bg_drain()
            self.barrier()

    def gates_bg(self, l, sb):
        gt = [sb(f"bggt{i}", [128, TB], BF16) for i in range(2)]
        wsl = [sb(f"bgws{i}", [128, NKT, 128], BF16) for i in range(2)]
        return self._gates_bg(l, gt, wsl)

    def _gates_bg(self, l, gt, wsl):
        kb = self.kb
        panels = [self.wm[l, n, mt] for n in range(3) for mt in range(16)]
        wst = WStream(self, panels, NKT, wsl, "bgws")
        gi = 0
        for i in range(48):
            n, mt = divmod(i, 16)
            w, k = wst.get(i)
            for tb in range(NTB):
                bi = self.bg_bank_i
                self.bg_bank_i ^= 1
                bank_idx = self.bg_banks[bi]
                ps, pkey = self.pm[bank_idx], ("pm", bank_idx)
                self.bg_open = True
                for kk in range(NKT):
                    self.MM(ps[:], w[:, kk, :], self.XB[:, kk, tb * TB:(tb + 1) * TB], kk == 0, kk == NKT - 1, [k, ("XB", kk)], [pkey])
                    if kk < NKT - 1:
                        yield
                g = gi % 2
                gi += 1
                self.A(gt[g][:], ps[:], AF.Sigmoid, ["vec"], [("bggt", g), pkey], bias=self.vcol("b_merge", n * 16 + mt))
                kb.dma(kb.q_sp, self.gscr[n, mt * 128:(mt + 1) * 128, tb * TB:(tb + 1) * TB], gt[g][:], reads=[("bggt", g)], writes=[("gscr", n, mt)])
                self.bg_open = False
                yield

    def bgs(self, n):
        g = getattr(self, "bg", None)
        if g is None:
            return
        for _ in range(n):
            try:
                next(g)
            except StopIteration:
                self.bg = None
                return

    def bg_close_group(self):
        while getattr(self, "bg", None) is not None and self.bg_open:
            self.bgs(1)

    def bg_drain(self):
        while getattr(self, "bg", None) is not None:
            self.bgs(64)

    def phase_gates(self, l):
        kb = self.kb
        self.barrier()
        es, sb = self.phase_alloc()
        with es:
            gtile = [sb(f"gtile{i}", [128, T], BF16) for i in range(2)]
            panels = [self.wm[l, n, mt] for n in range(3) for mt in range(16)]
            wst = WStream(self, panels, NKT, self.ws, "ws")
            for i in range(48):
                n, mt = divmod(i, 16)
                s = i % 2
                w, k = wst.get(i)

                def epi(tb, ps, pkey):
                    self.A(gtile[s][:, tb * TB:(tb + 1) * TB], ps[:], AF.Sigmoid, [pkey, "vec"], [("gtile", s, tb)],
                           bias=self.vcol("b_merge", n * 16 + mt))
                self.proj_xb(w, k, 128, epi)
                kb.dma(kb.q_sp, self.gscr[n, mt * 128:(mt + 1) * 128, :], gtile[s][:],
                       reads=[("gtile", s, tb) for tb in range(NTB)], writes=[("gscr", n, mt)])
            self.barrier()

    def x_src(self, l, s):
        return self.xT[s] if l == self.layers[0] else self.xres

    def phase_merge(self, l, s):
        kb = self.kb
        self.barrier()
        es, sb = self.phase_alloc()
        with es:
            Y = [sb(f"Y{i}", [128, 3, 8, TB], BF16) for i in range(2)]
            gl = [sb(f"gl{i}", [128, 3, TB], BF16) for i in range(3)]
            wbs = [sb(f"wbs{i}", [128, 8, 128], BF16) for i in range(6)]
            m0 = sb("m0", [128, TB])
            m1 = sb("m1", [128, TB])
            panels = [self.wb[l, n, mt] for _tb in range(NTB) for mt in range(16) for n in range(3)]
            wst = WStream(self, panels, 8, wbs, "wbs")
            ysrc = self.yin if getattr(self, "fake_y", False) else self.yscr
            pi = 0
            for tb in range(NTB):
                ys = tb % 2
                for n in range(3):
                    kb.dma(kb.q_sp, Y[ys][:, n, :, :], ysrc[n, :, tb * TB:(tb + 1) * TB].rearrange("(f p) t -> p f t", p=128),
                           reads=[("yscr", n, f) for f in range(8)], writes=[("Y", ys, n)])
                for mt in range(16):
                    g = (tb * 16 + mt) % 3
                    kb.dma(kb.q_sp, gl[g][:], self.gscr[:, mt * 128:(mt + 1) * 128, tb * TB:(tb + 1) * TB].rearrange("n p t -> p n t"),
                           reads=[("gscr", n, mt) for n in range(3)], writes=[("gl", g)])
                    for n in range(3):
                        w, k = wst.get(pi)
                        pi += 1
                        ps = self.pm[n]
                        for kk in range(8):
                            self.MM(ps[:], w[:, kk, :], Y[ys][:, n, kk, :], kk == 0, kk == 7, [k, ("Y", ys, n)], [("pm", n)])
                    self.TT(m0[:], self.pm[0][:], gl[g][:, 0, :], ALU.mult, [("pm", 0), ("gl", g)], ["m0"])
                    self.TT(m1[:], self.pm[1][:], gl[g][:, 1, :], ALU.mult, [("pm", 1), ("gl", g)], ["m1"])
                    self.TT(m0[:], m0[:], m1[:], ALU.add, ["m0", "m1"], ["m0"])
                    self.TT(m1[:], self.pm[2][:], gl[g][:, 2, :], ALU.mult, [("pm", 2), ("gl", g)], ["m1"])
                    self.TT(self.XB[:, mt, tb * TB:(tb + 1) * TB], m0[:], m1[:], ALU.add, ["m0", "m1"], [("XB", mt)], E=kb.pool)
            for mt in range(16):
                self.dump("merged", self.XB[:, mt, :], lambda o: o[mt * 128:(mt + 1) * 128, :], [("XB", mt)])
            xt = [sb(f"xt{i}", [128, TB]) for i in range(3)]
            zt = [sb(f"zt{i}", [128, TB]) for i in range(3)]
            xsrc = self.x_src(l, s)
            wst = WStream(self, [self.wo[l, mt] for mt in range(16)], NKT, self.ws, "ws")
            cnt = [0]
            for mt in range(16):
                w, k = wst.get(mt)

                def epi(tb, ps, pkey):
                    i = cnt[0] % 3
                    cnt[0] += 1
                    kb.dma(kb.q_sp, xt[i][:], xsrc[mt * 128:(mt + 1) * 128, tb * TB:(tb + 1) * TB], reads=[("xres", mt, tb)], writes=[("xt", i)])
                    self.STT(zt[i][:], xt[i][:], ALPHA, ps[:], ALU.mult, ALU.add, [("xt", i), pkey], [("zt", i)])
                    kb.dma(kb.q_sp, self.zscr[mt * 128:(mt + 1) * 128, tb * TB:(tb + 1) * TB], zt[i][:], reads=[("zt", i)], writes=[("zscr", mt, tb)])
                self.proj_xb(w, k, 128, epi)
            self.barrier()
        self.ln_pass(l, s, "ln1_g", "ln1_b", "x1", final=False)

    def ln_pass(self, l, s, gname, bname, dbgname, final):
        kb = self.kb
        self.barrier()
        es, sb = self.phase_alloc()
        with es:
            Zs = [sb(f"Z{i}", [128, NKT, TB]) for i in range(2)]
            sqt = [sb(f"sqt{i}", [128, TB]) for i in range(2)]
            mu = sb("mu", [128, TB])
            rstd = sb("rstd", [128, TB])
            ot = [sb(f"ot{i}", [128, TB]) for i in range(3)]
            pa, pb = self.pm[0], self.pm[1]
            oi = 0
            for tb in range(NTB):
                tsl = slice(tb * TB, (tb + 1) * TB)
                Z = Zs[tb % 2]
                zi = tb % 2
                for ft in range(NKT):
                    kb.dma(kb.q_sp, Z[:, ft, :], self.zscr[ft * 128:(ft + 1) * 128, tsl], reads=[("zscr", ft, tb)], writes=[("Z", zi, ft)])
                for ft in range(NKT):
                    self.MM(pa[:], self.onesf[:], Z[:, ft, :], ft == 0, ft == NKT - 1, ["onesf", ("Z", zi, ft)], ["pa"])
                self.A(mu[:], pa[:], AF.Copy, ["pa"], ["mu"], scale=1.0 / D)
                for ft in range(NKT):
                    self.TT(Z[:, ft, :], Z[:, ft, :], mu[:], ALU.subtract, [("Z", zi, ft), "mu"], [("Z", zi, ft)])
                    q = ft % 2
                    self.A(sqt[q][:], Z[:, ft, :], AF.Square, [("Z", zi, ft)], [("sqt", q)])
                    self.MM(pb[:], self.onesf[:], sqt[q][:], ft == 0, ft == NKT - 1, ["onesf", ("sqt", q)], ["pb"])
                self.A(rstd[:], pb[:], AF.Ln, ["pb"], ["rstd"], scale=1.0 / D, bias=LN_EPS)
                self.A(rstd[:], rstd[:], AF.Exp, ["rstd"], ["rstd"], scale=-0.5)
                for ft in range(NKT):
                    i = oi % 3
                    oi += 1
                    self.TT(Z[:, ft, :], Z[:, ft, :], rstd[:], ALU.mult, [("Z", zi, ft), "rstd"], [("Z", zi, ft)])
                    self.TS(ot[i][:], Z[:, ft, :], self.vcol(gname, ft), ALU.mult, [("Z", zi, ft), "vec"], [("ot", i)],
                            s2=self.vcol(bname, ft), op1=ALU.add, E=kb.pool)
                    self.A(self.XB[:, ft, tsl], ot[i][:], AF.Copy, [("ot", i)], [("XB", ft)])
                    rows = slice(ft * 128, (ft + 1) * 128)
                    if final:
                        kb.dma(kb.q_sp, self.outT[s, rows, tsl], ot[i][:], reads=[("ot", i)], writes=[("outT", s, ft, tb)])
                        self.out_keys.append(("outT", s, ft, tb))
                    else:
                        kb.dma(kb.q_sp, self.xres[rows, tsl], ot[i][:], reads=[("ot", i)], writes=[("xres", ft, tb)])
                    self.dump(dbgname, ot[i][:], lambda o: o[rows, tsl], [("ot", i)])
            self.barrier()

    def phase_ffn(self, l, s, final=False):
        kb = self.kb
        self.barrier()
        es, sb = self.phase_alloc()
        with es:
            gtp = [sb(f"gtp{i}", [128, T + 2]) for i in range(2)]
            up = [sb(f"up{i}", [128, T]) for i in range(2)]
            cv = sb("cv", [128, T])
            at = [sb(f"at{i}", [128, T], BF16) for i in range(2)]
            for i in range(2):
                kb.op(kb.dve, lambda e: e.memset(gtp[i][:, 0:2], 0.0), (), [("gtp", i, "h")])
            panels = []
            for f in range(NFF):
                panels += [self.wgt[l, f], self.wup[l, f]]
            wst = WStream(self, panels, NKT, self.ws, "ws")
            for f in range(NFF):
                q = f % 2
                w0, k0 = wst.get(2 * f)

                def epi_g(tb, ps, pkey):
                    self.A(gtp[q][:, 2 + tb * TB:2 + (tb + 1) * TB], ps[:], AF.Copy, [pkey], [("gtp", q, tb)])
                self.proj_xb(w0, k0, 128, epi_g)
                w1, k1 = wst.get(2 * f + 1)

                def epi_u(tb, ps, pkey):
                    self.A(up[q][:, tb * TB:(tb + 1) * TB], ps[:], AF.Copy, [pkey], [("up", q, tb)])
                self.proj_xb(w1, k1, 128, epi_u)
                gk = [("gtp", q, tb) for tb in range(NTB)] + [("gtp", q, "h")]
                uk = [("up", q, tb) for tb in range(NTB)]
                cw = lambda tap: self.vcol("ffn_cw", f * 3 + tap)
                self.TS(cv[:], gtp[q][:, 2:2 + T], cw(2), ALU.mult, gk + ["vec"], ["cv"], s2=self.vcol("ffn_cb", f), op1=ALU.add)
                for tap in (1, 0):
                    self.STT(cv[:], gtp[q][:, tap:tap + T], cw(tap), cv[:], ALU.mult, ALU.add, gk + ["cv", "vec"], ["cv"])
                self.A(cv[:], cv[:], AF.Silu, ["cv"], ["cv"])
                self.TT(at[q][:], cv[:], up[q][:], ALU.mult, ["cv"] + uk, [("at", q)], E=kb.pool)
                kb.dma(kb.q_sp, self.ascr[f * 128:(f + 1) * 128, :], at[q][:], reads=[("at", q)], writes=[("ascr", f)])
                self.dump("act", at[q][:], lambda o: o[f * 128:(f + 1) * 128, :], [("at", q)])
            self.barrier()
        es, sb = self.phase_alloc()
        with es:
            pTb = sb("pTb", [128, 2, T], BF16)
            sgt = [sb(f"sgt{i}", [128, T]) for i in range(2)]
            plt = [sb(f"plt{i}", [128, T]) for i in range(2)]
            wpps = [sb(f"wpps{i}", [128, 2, 128], BF16) for i in range(2)]
            for f in range(2):
                kb.dma(kb.q_pool, pTb[:, f, :], self.pT[l, s, f * 128:(f + 1) * 128, :], writes=[("pTb", f)])
            wst = WStream(self, [self.wpg[l, mt] for mt in range(16)], NKT, self.ws, "ws")
            wst2 = WStream(self, [self.wpp[l, mt] for mt in range(16)], 2, wpps, "wpps")
            for mt in range(16):
                q = mt % 2
                w, k = wst.get(mt)

                def epi_s(tb, ps, pkey):
                    self.A(sgt[q][:, tb * TB:(tb + 1) * TB], ps[:], AF.Sigmoid, [pkey], [("sgt", q, tb)])
                self.proj_xb(w, k, 128, epi_s)
                w2, k2 = wst2.get(mt)
                for tb in range(NTB):
                    ps, pkey = self.next_pg()
                    for kk in range(2):
                        self.MM(ps[:], w2[:, kk, :], pTb[:, kk, tb * TB:(tb + 1) * TB], kk == 0, kk == 1, [k2, ("pTb", kk)], [pkey])
                    self.TT(plt[q][:, tb * TB:(tb + 1) * TB], ps[:], sgt[q][:, tb * TB:(tb + 1) * TB], ALU.mult,
                            [pkey, ("sgt", q, tb)], [("plt", q, tb)])
                kb.dma(kb.q_sp, self.zscr[mt * 128:(mt + 1) * 128, :], plt[q][:], reads=[("plt", q, tb) for tb in range(NTB)],
                       writes=[("zscr", mt, tb) for tb in range(NTB)])
            self.barrier()
        es, sb = self.phase_alloc()
        with es:
            TH = 2 * TB
            AT = sb("AT", [128, NFF, TH], BF16)
            wds = [sb(f"wds{i}", [128, NFF, 128], BF16) for i in range(2)]
            xt = [sb(f"xt{i}", [128, TB]) for i in range(2)]
            pl = [sb(f"pl{i}", [128, TB]) for i in range(2)]
            wst = WStream(self, [self.wdn[l, mt] for _h in range(2) for mt in range(16)], NFF, wds, "wds")
            xsrc = self.x1in if getattr(self, "fake_x1", False) else self.xres
            cnt = 0
            for hf in range(2):
                hsl = slice(hf * TH, (hf + 1) * TH)
                for f0 in range(0, NFF, 11):
                    kb.dma(kb.q_sp, AT[:, f0:f0 + 11, :], self.ascr[f0 * 128:(f0 + 11) * 128, hsl].rearrange("(f p) t -> p f t", p=128),
                           reads=[("ascr", f) for f in range(f0, f0 + 11)], writes=[("AT", f0)])
                for mt in range(16):
                    rows = slice(mt * 128, (mt + 1) * 128)
                    w, k = wst.get(hf * 16 + mt)
                    for t2 in range(2):
                        tb = hf * 2 + t2
                        tsl = slice(tb * TB, (tb + 1) * TB)
                        i = cnt % 2
                        cnt += 1
                        ps, pkey = self.next_pg()
                        for kk in range(NFF):
                            self.MM(ps[:], w[:, kk, :], AT[:, kk, t2 * TB:(t2 + 1) * TB], kk == 0, kk == NFF - 1, [k, ("AT", (kk // 11) * 11)], [pkey])
                        kb.dma(kb.q_sp, xt[i][:], xsrc[rows, tsl], reads=[("xres", mt, tb)], writes=[("xt", i)])
                        kb.dma(kb.q_sp, pl[i][:], self.zscr[rows, tsl], reads=[("zscr", mt, tb)], writes=[("pl", i)])
                        self.STT(xt[i][:], xt[i][:], ALPHA, ps[:], ALU.mult, ALU.add, [("xt", i), pkey], [("xt", i)])
                        self.TT(xt[i][:], xt[i][:], pl[i][:], ALU.add, [("xt", i), ("pl", i)], [("xt", i)], E=kb.pool)
                        kb.dma(kb.q_sp, self.zscr[rows, tsl], xt[i][:], reads=[("xt", i)], writes=[("zscr", mt, tb)])
            self.barrier()
        self.ln_pass(l, s, "ln2_g", "ln2_b", "x2", final=final)

    def phase_gdn(self, l):
        kb = self.kb
        self.barrier()
        es, sb = self.phase_alloc()
        with es:
            gam8 = sb("gam8", [8, T])
            TM = sb("TM", [128, 16, 4, 8])
            Esel = sb("Esel", [8, 8, 128])
            negm = sb("negm", [128, 128])
            pmA, pmG, pmU, pmW, pmO = self.pm
            for h in range(8):
                self.CP(Esel[:, h, :], self.con[0:8, C_ID + h:C_ID + h + 1].to_broadcast([8, 128]), ["con"], [("Esel", h)])
            self.TS(negm[:], self.con[:, C_MS:C_MS + 128], -1.0, ALU.mult, ["con"], ["negm"])
            es2, sb2 = self.phase_alloc()
            with es2:
                da8, bt8, c1_8, c2_8, eg8 = [sb2(n, [8, T]) for n in ("da8", "bt8", "c1_8", "c2_8", "eg8")]
                wst = WStream(self, [self.win[l, MT_DAB], self.win[l, MT_DB]], NKT, self.ws, "ws")
                w0, k0 = wst.get(0)
                tbs = lambda tb: slice(tb * TB, (tb + 1) * TB)

                def epi_a(tb, ps, pkey):
                    self.A(da8[:, tbs(tb)], ps[0:8, :], AF.Exp, [pkey, "vec"], [("da8", tb)], bias=self.vcol("gdn_dtb", 0, 8))
                    self.A(da8[:, tbs(tb)], da8[:, tbs(tb)], AF.Ln, [("da8", tb)], [("da8", tb)], bias=1.0)
                    self.TS(da8[:, tbs(tb)], da8[:, tbs(tb)], self.der[0:8, 20:21], ALU.mult, [("da8", tb), "der"], [("da8", tb)])
                self.proj_xb(w0, k0, 8, epi_a)
                w1, k1 = wst.get(1)

                def epi_b(tb, ps, pkey):
                    self.A(bt8[:, tbs(tb)], ps[0:8, :], AF.Sigmoid, [pkey], [("bt8", tb)])
                self.proj_xb(w1, k1, 8, epi_b)
                dk_ = [("da8", tb) for tb in range(NTB)]
                bk_ = [("bt8", tb) for tb in range(NTB)]
                kb.op(kb.dve, lambda e: e.tensor_tensor_scan(out=gam8[:], data0=self.con[0:8, C_CM:C_CM + T], data1=da8[:], initial=0.0,
                                                             op0=ALU.mult, op1=ALU.add), dk_ + ["con"], ["gam8"])
                self.A(eg8[:], gam8[:], AF.Exp, ["gam8"], ["eg8"])
                self.TT(c1_8[:], bt8[:], eg8[:], ALU.mult, bk_ + ["eg8"], ["c1_8"])
                self.TT(c2_8[:].rearrange("p (c t) -> p c t", t=64), gam8[:, 63:T:64].unsqueeze(2).to_broadcast([8, 32, 64]),
                        gam8[:].rearrange("p (c t) -> p c t", t=64), ALU.subtract, ["gam8"], ["c2_8"])
                self.A(c2_8[:], c2_8[:], AF.Exp, ["c2_8"], ["c2_8"])
                for tt in range(16):
                    cs = slice(tt * 128, (tt + 1) * 128)
                    for qi, (src, sk) in enumerate(((gam8, ["gam8"]), (bt8, bk_), (c1_8, ["c1_8"]), (c2_8, ["c2_8"]))):
                        kb.op(kb.pe, lambda e: e.transpose(out=pmA[:, qi * 8:(qi + 1) * 8], in_=src[:, cs], identity=self.con[0:8, C_ID:C_ID + 8]),
                              sk + ["con"], ["pmA"])
                    self.A(TM[:, tt, :, :], pmA[:, 0:32].rearrange("p (a b) -> p a b", a=4), AF.Copy, ["pmA"], [("TM", tt)])
                self.barrier()
            if getattr(self, "gdn_pre_only", False):
                return
            xp = sb("xp", [128, T + 4])
            cf = sb("cf", [128, T])
            rn = sb("rn", [128, TB])
            rn2 = sb("rn2", [128, TB])
            sqt1 = sb("sqt1", [128, TB], BF16)
            sqt2 = sb("sqt2", [128, TB], BF16)
            qnb, qdec, knb, cvb, sgd = [[sb(f"{n}{p}", [128, T], BF16) for p in range(2)] for n in ("qnb", "qdec", "knb", "cvb", "sgd")]
            lastc = [sb(f"lastc{p}", [128, 32]) for p in range(2)]
            O = sb("O", [128, T])
            G4 = 4
            KDC, QKT, WTn, USB = [sb(n, [128, 16, 128], BF16) for n in ("KDC", "QKT", "WTn", "USB")]
            TMn = sb("TMn", [128, 16, 8])
            kbg4, vbt4 = [sb(n, [128, 4, 128], BF16) for n in ("kbg4", "vbt4")]
            Yr4 = [sb(f"Yr4{i}", [128, 4, 128], BF16) for i in range(2)]
            Qr4 = [sb(f"Qr4{i}", [128, 4, 128], BF16) for i in range(2)]
            Mt4 = [sb(f"Mt4{i}", [128, 4, 128], BF16) for i in range(2)]
            ndf4, ndc4 = [sb(n, [128, 4, 128]) for n in ("ndf4", "ndc4")]
            vn = [sb(f"vn{i}", [128, 128], BF16) for i in range(2)]
            S = sb("S", [128, 128])
            Sb = [sb(f"Sb{i}", [128, 128], BF16) for i in range(4)]
            self.TS(TMn[:], TM[:, :, 0, :], -1.0, ALU.mult, [("TM", tt) for tt in range(16)], ["TMn"])
            kb.op(kb.dve, lambda e: e.memset(xp[:, 0:4], 0.0), (), [("xp", "h")])
            panels = []
            for h in range(8):
                panels += [self.win[l, MT_DQ + h], self.win[l, MT_DK + h], self.win[l, MT_DV + h], self.win[l, MT_DOG + h]]
            wst = WStream(self, panels, NKT, self.ws, "ws")
            xk = [("xp", tb) for tb in range(NTB)] + [("xp", "h")]
            NH = getattr(self, 'gdn_heads', 8)

            def S0(h):
                p = h % 2

                def proj_g(w, k, epi):
                    for tb in range(NTB):
                        ps, pkey = self.next_pg()
                        for kk in range(NKT):
                            self.MM(ps[:], w[:, kk, :], self.XB[:, kk, tbs(tb)], kk == 0, kk == NKT - 1, [k, ("XB", kk)], [pkey])
                            if kk % 4 == 3 and kk < NKT - 1:
                                yield
                        epi(tb, ps, pkey)
                        yield

                def conv_silu_g(pi, ft, out_ap, out_keys):
                    w, k = wst.get(pi)

                    def epi(tb, ps, pkey):
                        self.A(xp[:, 4 + tb * TB:4 + (tb + 1) * TB], ps[:], AF.Copy, [], [("xp", tb), pkey])
                    yield from proj_g(w, k, epi)
                    cw = lambda tap: self.vcol("gdn_cw", ft * 4 + tap)
                    self.TS(cf[:], xp[:, 4:4 + T], cw(3), ALU.mult, xk + ["vec"], ["cf"])
                    yield
                    for tap in (2, 1, 0):
                        self.STT(cf[:], xp[:, 1 + tap:1 + tap + T], cw(tap), cf[:], ALU.mult, ALU.add, xk + ["cf", "vec"], ["cf"])
                        yield
                    self.A(out_ap, cf[:], AF.Silu, ["cf"], out_keys)
                    yield

                def l2n_g(scale, dst_ap, dst_keys):
                    for tb in range(NTB):
                        self.A(sqt1[:], cf[:, tbs(tb)], AF.Square, ["cf"], ["sqt1"])
                        ps, pkey = self.next_pg()
                        self.MM(ps[:], self.onesb[:], sqt1[:], True, True, ["onesb", "sqt1"], [pkey])
                        self.A(rn[:], ps[:], AF.Ln, [], ["rn", pkey], bias=NORM_EPS)
                        self.A(rn[:], rn[:], AF.Exp, ["rn"], ["rn"], scale=-0.5)
                        self.STT(cf[:, tbs(tb)], cf[:, tbs(tb)], scale, rn[:], ALU.mult, ALU.mult, ["cf", "rn"], ["cf"])
                        yield
                    self.CP(dst_ap, cf[:], ["cf"], dst_keys, E=kb.pool)
                    yield

                yield from conv_silu_g(4 * h, h, cf[:], ["cf"])
                yield from l2n_g(128 ** -0.5, qnb[p][:], [("qnb", p)])
                for tb in range(NTB):
                    ps, pkey = self.next_pg()
                    self.MM(ps[:], Esel[:, h, :], gam8[:, tbs(tb)], True, True, [("Esel", h), "gam8"], [pkey])
                    self.A(lastc[p][:, tb * 8:(tb + 1) * 8], ps[:, 63:TB:64], AF.Exp, [], [("lastc", p), pkey])
                    self.A(rn[:], ps[:], AF.Exp, [], ["rn", pkey])
                    self.TT(qdec[p][:, tbs(tb)], cf[:, tbs(tb)], rn[:], ALU.mult, ["cf", "rn"], [("qdec", p)])
                    yield
                yield from conv_silu_g(4 * h + 1, 8 + h, cf[:], ["cf"])
                yield from l2n_g(1.0, knb[p][:], [("knb", p)])
                yield from conv_silu_g(4 * h + 2, 16 + h, cvb[p][:], [("cvb", p)])
                wg, kg = wst.get(4 * h + 3)

                def epi_g(tb, ps, pkey):
                    self.A(sgd[p][:, tbs(tb)], ps[:], AF.Silu, [], [("sgd", p, tb), pkey])
                yield from proj_g(wg, kg, epi_g)

            Xb = self.pm[0][:].bitcast(BF16)
            Yb, Zb, Wb = self.pm[1][:], self.pm[2][:], self.pm[3][:]
            kX, kY, kZ, kW = ("pm", 0), ("pm", 1), ("pm", 2), ("pm", 3)
            v3 = lambda ap: ap.rearrange("p (a b) -> p a b", b=128)
            bc = lambda ap2: ap2.unsqueeze(1).to_broadcast([128, 4, 128])
            idf = self.con[:, C_ID:C_ID + 128]
            R = range(G4)
            reg = lambda r: slice(r * 128, (r + 1) * 128)
            pU, pS = self.pm[4][:], self.pt[:].bitcast(F32)
            kU, kS = ("pm", 4), "pt"

            def rr_g(*gens):
                gens = [g for g in gens if g is not None]
                while gens:
                    for g in list(gens):
                        try:
                            next(g)
                        except StopIteration:
                            gens.remove(g)
                        yield

            def S12(h):
                p = h % 2
                knb_, qnb_, cvb_, qdec_, sgd_, lastc_ = knb[p], qnb[p], cvb[p], qdec[p], sgd[p], lastc[p]
                kK, kQ, kV, kQD = ("knb", p), ("qnb", p), ("cvb", p), ("qdec", p)

                def s1(g0):
                    gs = slice(g0, g0 + G4)
                    cs = [slice((g0 + i) * 128, (g0 + i + 1) * 128) for i in R]
                    tmk = [("TM", g0 + i) for i in R]
                    colb = lambda q: TM[:, gs, q, h].unsqueeze(2).to_broadcast([128, 4, 128])
                    gk = lambda nm: [(nm, g0 + i) for i in R]
                    for i in R:
                        self.PT(Xb[:, reg(i)], knb_[:, cs[i]], [kK], [kX])
                        self.PT(Xb[:, reg(4 + i)], cvb_[:, cs[i]], [kV], [kX])
                    for i in R:
                        self.MM(Yb[:, reg(i)], knb_[:, cs[i]], knb_[:, cs[i]], True, True, [kK], [kY])
                        self.MM(Zb[:, reg(i)], knb_[:, cs[i]], qnb_[:, cs[i]], True, True, [kK, kQ], [kZ])
                        self.MM(Wb[:, reg(i)], Esel[:, h, :], gam8[:, cs[i]], True, True, [("Esel", h), "gam8"], [kW])
                    yield
                    self.TT(kbg4[:], v3(Xb[:, 0:512]), colb(2), ALU.mult, tmk, ["kbg4", kX])
                    self.TT(KDC[:, gs, :], v3(Xb[:, 0:512]), colb(3), ALU.mult, tmk, gk("KDC") + [kX])
                    self.TT(vbt4[:], v3(Xb[:, 512:1024]), colb(1), ALU.mult, tmk, ["vbt4", kX])
                    self.TT(ndf4[:], v3(Wb), TMn[:, gs, h].unsqueeze(2).to_broadcast([128, 4, 128]), ALU.add, ["TMn"], ["ndf4", kW])
                    self.TS(ndc4[:], ndf4[:], 0.0, ALU.max, ["ndf4"], ["ndc4"])
                    self.TS(ndf4[:], ndf4[:], 0.0, ALU.min, ["ndf4"], ["ndf4"])
                    self.A(ndc4[:], ndc4[:], AF.Exp, ["ndc4"], ["ndc4"], scale=-1.0)
                    self.A(ndf4[:], ndf4[:], AF.Exp, ["ndf4"], ["ndf4"])
                    self.TT(ndc4[:], v3(Yb), ndc4[:], ALU.mult, ["ndc4"], ["ndc4", kY])
                    self.TT(ndc4[:], ndc4[:], colb(1), ALU.mult, ["ndc4"] + tmk, ["ndc4"])
                    self.TT(Yr4[0][:], ndc4[:], bc(negm[:]), ALU.mult, ["ndc4", "negm"], [("Yr4", 0)], E=kb.pool)
                    self.TT(ndf4[:], v3(Zb), ndf4[:], ALU.mult, ["ndf4"], ["ndf4", kZ])
                    self.TT(QKT[:, gs, :], ndf4[:], bc(self.con[:, C_MIT:C_MIT + 128]), ALU.mult, ["ndf4", "con"], gk("QKT"), E=kb.pool)
                    for i in R:
                        self.PT(Xb[:, reg(i)], Yr4[0][:, i, :], [("Yr4", 0)], [kX])
                    self.A(Qr4[0][:], v3(Xb[:, 0:512]), AF.Copy, [], [("Qr4", 0), kX])
                    self.TT(Mt4[0][:], Qr4[0][:], bc(idf), ALU.add, [("Qr4", 0), "con"], [("Mt4", 0)], E=kb.pool)
                    yield
                    yc = qc = mc = 0
                    for j in range(1, 6):
                        for i in R:
                            self.MM(Yb[:, reg(i)], Qr4[qc][:, i, :], Yr4[yc][:, i, :], True, True, [("Qr4", qc), ("Yr4", yc)], [kY])
                        if j <= 4:
                            for i in R:
                                self.MM(Zb[:, reg(i)], Yr4[yc][:, i, :], Qr4[qc][:, i, :], True, True, [("Qr4", qc), ("Yr4", yc)], [kZ])
                        self.A(Yr4[yc ^ 1][:], v3(Yb), AF.Copy, [], [("Yr4", yc ^ 1), kY])
                        if j <= 4:
                            self.CP(Qr4[qc ^ 1][:], v3(Zb), [], [("Qr4", qc ^ 1), kZ])
                        yc ^= 1
                        if j <= 4:
                            qc ^= 1
                        for i in R:
                            self.MM(Wb[:, reg(i)], self.identb[:], Mt4[mc][:, i, :], i == 0, False, ["identb", ("Mt4", mc)], [kW])
                        for i in R:
                            self.MM(Wb[:, reg(i)], Yr4[yc][:, i, :], Mt4[mc][:, i, :], False, True, [("Yr4", yc), ("Mt4", mc)], [kW])
                        self.A(Mt4[mc ^ 1][:], v3(Wb), AF.Copy, [], [("Mt4", mc ^ 1), kW])
                        mc ^= 1
                        yield
                    for i in R:
                        self.MM(Yb[:, reg(i)], Mt4[mc][:, i, :], vbt4[:, i, :], True, True, [("Mt4", mc), "vbt4"], [kY])
                        self.MM(Zb[:, reg(i)], kbg4[:, i, :], Mt4[mc][:, i, :], True, True, [("Mt4", mc), "kbg4"], [kZ])
                    self.A(USB[:, gs, :], v3(Yb), AF.Copy, [], gk("USB") + [kY])
                    self.A(WTn[:, gs, :], v3(Zb), AF.Copy, [], gk("WTn") + [kZ], scale=-1.0)
                    yield

                kb.op(kb.dve, lambda e: e.memset(S[:], 0.0), (), ["S"])
                kb.op(kb.dve, lambda e: e.memset(Sb[0][:], 0.0), (), [("Sb", 0)])
                st = {"si": 0}

                def s2(g0):
                    for pr in range(g0, g0 + G4):
                        cols = slice(pr * 128, (pr + 1) * 128)
                        v = pr % 2
                        sbu = []
                        for c in range(2):
                            si = st["si"]
                            pp = slice(c * 64, (c + 1) * 64)
                            self.MM(pU[pp, 0:128], self.identb[:, pp], USB[:, pr, :], True, False, ["identb", ("USB", pr)], [kU])
                            self.MM(pU[pp, 0:128], WTn[:, pr, pp], Sb[si][:], False, True, [("WTn", pr), ("Sb", si)], [kU])
                            self.A(vn[v][pp, :], pU[pp, 0:128], AF.Copy, [], [("vn", v), kU])
                            sbu.append(si)
                            self.MM(pS[:, 0:128], KDC[pp, pr, :], vn[v][pp, :], True, True, [("KDC", pr), ("vn", v)], [kS])
                            ci = pr * 2 + c
                            self.STT(S[:], S[:], lastc_[:, ci:ci + 1], pS[:, 0:128], ALU.mult, ALU.add, ["S", ("lastc", p)], ["S", kS])
                            si = (si + 1) % 4
                            st["si"] = si
                            self.A(Sb[si][:], S[:], AF.Copy, ["S"], [("Sb", si)])
                            yield
                        self.MM(pU[:, 128:256], vn[v][:], QKT[:, pr, :], True, False, [("vn", v), ("QKT", pr)], [kU])
                        for c in range(2):
                            self.MM(pU[:, 128 + c * 64:128 + (c + 1) * 64], Sb[sbu[c]][:], qdec_[:, pr * 128 + c * 64:pr * 128 + (c + 1) * 64],
                                    False, c == 1, [("Sb", sbu[c]), kQD], [kU])
                        self.A(O[:, cols], pU[:, 128:256], AF.Copy, [], [("O", pr), kU])

                yield from rr_g(s1(0))
                for g in range(4):
                    yield from rr_g(s1((g + 1) * G4) if g < 3 else None, s2(g * G4))
                for tb in range(NTB):
                    ok = [("O", pr) for pr in range(tb * 4, tb * 4 + 4)]
                    self.A(sqt2[:], O[:, tbs(tb)], AF.Square, ok, ["sqt2"])
                    self.MM(Wb[:], self.onesb[:], sqt2[:], True, True, ["onesb", "sqt2"], [kW])
                    self.A(rn2[:], Wb[:], AF.Ln, [], ["rn2", kW], scale=1.0 / 128.0, bias=NORM_EPS)
                    self.A(rn2[:], rn2[:], AF.Exp, ["rn2"], ["rn2"], scale=-0.5)
                    self.STT(O[:, tbs(tb)], O[:, tbs(tb)], self.vcol("gdn_ng", 0), rn2[:], ALU.mult, ALU.mult, ok + ["vec", "rn2"], ok)
                    self.TT(qnb_[:, tbs(tb)], O[:, tbs(tb)], sgd_[:, tbs(tb)], ALU.mult, ok + [("sgd", p, tb), kQ], [("ytc", tb), kQ])
                    yield
                yk = [("ytc", tb) for tb in range(NTB)]
                kb.dma(kb.q_sp, self.yscr[2, h * 128:(h + 1) * 128, :], qnb_[:], reads=yk + [kQ], writes=[("yscr", 2, h)])
                self.dump("yc", qnb_[:], lambda o: o[h * 128:(h + 1) * 128, :], yk + [kQ])

            def run_rr(*gens):
                for _ in rr_g(*gens):
                    pass

            run_rr(S0(0))
            for h in range(NH):
                run_rr(S0(h + 1) if h + 1 < NH else None, S12(h))
            self.barrier()


def build_program(n_seq=SEQ_PER_CORE, layers=tuple(range(DEPTH)), lite=False):
    P = Prog(n_seq, list(layers), lite=lite)
    P.setup()
    for s in range(n_seq):
        P.load_x0(s)
        for l in layers:
            P.layer_setup(l)
            P.barrier()
            bg_es, bg_sb = P.phase_alloc()
            with bg_es:
                P.bg_bank_i = 0
                P.bg_open = False
                P.bg = P.gates_bg(l, bg_sb)
                P.bg_banks = [0, 1]
                P.phase_lru(l)
                P.bg_banks = [3, 4]
                P.phase_gla(l)
                P.bg = None
            P.phase_gdn(l)
            P.phase_merge(l, s)
            P.phase_ffn(l, s, final=(l == layers[-1]))
    P.barrier()
    P.kb.close()
    return P


def kernel(**inputs):
    inp = {k: np.asarray(v) for k, v in inputs.items()}
    W = prep_weights(inp)
    x = inp["x"]
    p = inp["p"]
    B = x.shape[0]
    n_seq = B // N_CORES
    P = build_program(n_seq)
    in_maps = []
    for c in range(N_CORES):
        m = dict(W)
        bs = slice(c * n_seq, (c + 1) * n_seq)
        m["xT"] = np.ascontiguousarray(x[bs].transpose(0, 2, 1))
        m["pT"] = np.ascontiguousarray(p[:, bs].transpose(0, 1, 3, 2))
        in_maps.append(m)
    res = run_bass_kernel_spmd(P.nc, in_maps, core_ids=list(range(N_CORES)))
    out = np.empty((B, T, D), np.float32)
    for c in range(N_CORES):
        o = np.asarray(res.results[c]["outT"])
        out[c * n_seq:(c + 1) * n_seq] = o.transpose(0, 2, 1)
    return out
```

```python
import contextlib
import numpy as np
import concourse.bass as bass
import concourse.mybir as mybir
from concourse.bass_utils import run_bass_kernel_spmd

F32 = mybir.dt.float32
BF16 = mybir.dt.bfloat16
AF = mybir.ActivationFunctionType
ALU = mybir.AluOpType

D = 2048
T = 2048
NKT = 16
NTB = 4
TB = 512
DEPTH = 4
DFF = 5632
NFF = 44
ALPHA = (2 * DEPTH) ** 0.25
LN_EPS = 1e-5
NORM_EPS = 1e-6
N_CORES = 8
SEQ_PER_CORE = 2

IN_COLS = dict(lru_x=0, lru_gate=1024, gq=2048, gk=2560, gv=3072, glr=4096, gog=4112,
               dq=5136, dk=6160, dv=7184, da=8208, db=8216, dog=8224)
MT_LRUX, MT_LRUG, MT_GQ, MT_GK, MT_GV, MT_GOG = 0, 8, 16, 20, 24, 32
MT_DQ, MT_DK, MT_DV, MT_DOG, MT_GLR, MT_DAB, MT_DB = 40, 48, 56, 64, 72, 73, 74
N_IN_MT = 75

VC = {}
_o = 0
for _n, _w in [("lru_cw", 32), ("lru_cb", 8), ("lru_br", 8), ("lru_bi", 8), ("lru_lam", 8), ("gla_bg2", 4),
               ("gla_ng", 2), ("gdn_cw", 96), ("gdn_ng", 1), ("b_merge", 48), ("ln1_g", 16), ("ln1_b", 16),
               ("ln2_g", 16), ("ln2_b", 16), ("ffn_cw", 132), ("ffn_cb", 44), ("gdn_alog", 1), ("gdn_dtb", 1)]:
    VC[_n] = _o
    _o += _w
NV = _o

C_ID, C_MIT, C_MST, C_MS, C_CM = 0, 128, 256, 384, 512
NCON = 512 + T


def make_consts():
    c = np.zeros((128, NCON), np.float32)
    i = np.arange(128)
    same = (i[:, None] // 64) == (i[None, :] // 64)
    c[:, C_ID:C_ID + 128] = np.eye(128)
    c[:, C_MIT:C_MIT + 128] = same & (i[:, None] <= i[None, :])
    c[:, C_MST:C_MST + 128] = same & (i[:, None] < i[None, :])
    c[:, C_MS:C_MS + 128] = same & (i[:, None] > i[None, :])
    cm = np.ones(T, np.float32)
    cm[::64] = 0.0
    c[:, C_CM:] = cm[None, :]
    return c


def panelize(w, nk):
    K, M = w.shape
    return np.ascontiguousarray(w.reshape(nk, 128, M // 128, 128).transpose(2, 1, 0, 3))


def prep_weights(inp):
    L = DEPTH
    out = {}
    w_in = inp["w_in"]
    win_p = np.zeros((L, N_IN_MT, 128, NKT, 128), np.float32)
    for l in range(L):
        w = w_in[l]
        order = [("lru_x", 8), ("lru_gate", 8), ("gq", 4), ("gk", 4), ("gv", 8), ("gog", 8),
                 ("dq", 8), ("dk", 8), ("dv", 8), ("dog", 8)]
        mt = 0
        for name, n in order:
            c0 = IN_COLS[name]
            win_p[l, mt:mt + n] = panelize(w[:, c0:c0 + n * 128], NKT)
            mt += n
        wk = w.reshape(NKT, 128, -1)
        win_p[l, MT_GLR, :, :, 0:16] = wk[:, :, 4096:4112].transpose(1, 0, 2)
        win_p[l, MT_DAB, :, :, 0:8] = wk[:, :, 8208:8216].transpose(1, 0, 2)
        win_p[l, MT_DB, :, :, 0:8] = wk[:, :, 8216:8224].transpose(1, 0, 2)
    out["win_p"] = win_p
    out["wm_p"] = np.stack([np.stack([panelize(inp["w_merge"][l, n], 16) for n in range(3)]) for l in range(L)])
    out["wb_p"] = np.stack([np.stack([panelize(inp["w_branch"][l, n], 8) for n in range(3)]) for l in range(L)])
    out["wo_p"] = np.stack([panelize(inp["w_out"][l], 16) for l in range(L)])
    out["wup_p"] = np.stack([panelize(inp["ffn_w_up"][l], 16) for l in range(L)])
    out["wgt_p"] = np.stack([panelize(inp["ffn_w_gate"][l], 16) for l in range(L)])
    out["wdn_p"] = np.stack([panelize(inp["ffn_w_down"][l], NFF) for l in range(L)])
    out["wpg_p"] = np.stack([panelize(inp["ple_w_gate"][l], 16) for l in range(L)])
    out["wpp_p"] = np.stack([panelize(inp["ple_w_proj"][l], 2) for l in range(L)])
    bd = np.zeros((L, 2, 8, 128, 128), np.float32)
    for l in range(L):
        for gi, nm in enumerate(("lru_wr", "lru_wi")):
            w = inp[nm][l]
            for g in range(16):
                j, o = g // 2, (g % 2) * 64
                bd[l, gi, j, o:o + 64, o:o + 64] = w[g]
    out["bd"] = bd
    out["wg2"] = np.ascontiguousarray(inp["gla_wg2"])
    v = np.zeros((L, 128, NV), np.float32)

    def put(name, arr, nft):
        v[:, :, VC[name]:VC[name] + nft] = arr.reshape(L, nft, 128).transpose(0, 2, 1)

    v[:, :, VC["lru_cw"]:VC["lru_cw"] + 32] = inp["lru_conv_w"].reshape(L, 4, 8, 128).transpose(0, 3, 2, 1).reshape(L, 128, 32)
    put("lru_cb", inp["lru_conv_b"], 8)
    put("lru_br", inp["lru_br"], 8)
    put("lru_bi", inp["lru_bi"], 8)
    put("lru_lam", inp["lru_lambda"], 8)
    put("gla_bg2", inp["gla_bg2"], 4)
    put("gla_ng", inp["gla_norm_g"], 2)
    v[:, :, VC["gdn_cw"]:VC["gdn_cw"] + 96] = inp["gdn_conv_w"].reshape(L, 4, 24, 128).transpose(0, 3, 2, 1).reshape(L, 128, 96)
    put("gdn_ng", inp["gdn_norm_g"], 1)
    put("b_merge", inp["b_merge"].reshape(L, 3 * D), 48)
    put("ln1_g", inp["ln1_g"], 16)
    put("ln1_b", inp["ln1_b"], 16)
    put("ln2_g", inp["ln2_g"], 16)
    put("ln2_b", inp["ln2_b"], 16)
    v[:, :, VC["ffn_cw"]:VC["ffn_cw"] + 132] = inp["ffn_conv_w"].reshape(L, 3, NFF, 128).transpose(0, 3, 2, 1).reshape(L, 128, 132)
    put("ffn_cb", inp["ffn_conv_b"], NFF)
    v[:, 0:8, VC["gdn_alog"]] = inp["gdn_a_log"]
    v[:, 0:8, VC["gdn_dtb"]] = inp["gdn_dt_bias"]
    out["vecs"] = v
    out["consts"] = make_consts()
    return out


class Eng:
    def __init__(self, name, eng, sem):
        self.name, self.eng, self.sem = name, eng, sem
        self.count = 0
        self.waited = {}


class DmaQ:
    def __init__(self, name, eng, sems, host):
        self.name, self.eng, self.sems, self.host = name, eng, sems, host
        self.totals = [0] * len(sems)
        self.idx = 0
        self.waited = {}


class KB:
    def __init__(self, nc, n_dma_sems=24):
        self.nc = nc
        self.es = contextlib.ExitStack()
        self.last_write = {}
        self.readers = {}
        self.n_inst = 0
        self.n_wait = 0
        mk = lambda n: self.es.enter_context(nc.semaphore(n))
        self.pe = Eng("pe", nc.tensor, mk("s_pe"))
        self.act = Eng("act", nc.scalar, mk("s_act"))
        self.dve = Eng("dve", nc.vector, mk("s_dve"))
        self.pool = Eng("pool", nc.gpsimd, mk("s_pool"))
        self.q_sp = DmaQ("q_sp", nc.sync, [mk(f"s_qsp{i}") for i in range(n_dma_sems)], None)
        self.q_pool = DmaQ("q_pool", nc.gpsimd, [mk(f"s_qpl{i}") for i in range(n_dma_sems)], self.pool)

    def sbuf(self, name, shape, dtype):
        return self.es.enter_context(self.nc.sbuf_tensor(name, list(shape), dtype))

    def psum(self, name, shape, dtype=F32):
        return self.es.enter_context(self.nc.psum_tensor(name, list(shape), dtype))

    def close(self):
        self.es.close()

    def _deps(self, reads, writes):
        deps = {}

        def add(d):
            for s, v in d.items():
                if deps.get(s, (None, 0))[1] < v[1]:
                    deps[s] = v

        for k in reads:
            add(self.last_write.get(k, {}))
        for k in writes:
            add(self.last_write.get(k, {}))
            add(self.readers.get(k, {}))
        return deps

    def _emit_waits(self, waiter, eng, deps):
        for sid, (sem, val) in deps.items():
            if waiter.waited.get(sid, 0) >= val:
                continue
            eng.wait_ge(sem, val)
            self.n_wait += 1
            waiter.waited[sid] = val

    def _record(self, ev, reads, writes):
        sid, sem, val = ev
        for k in reads:
            r = self.readers.setdefault(k, {})
            if r.get(sid, (None, 0))[1] < val:
                r[sid] = (sem, val)
        for k in writes:
            self.last_write[k] = {sid: (sem, val)}
            self.readers[k] = {}

    def op(self, E, fn, reads=(), writes=()):
        deps = self._deps(reads, writes)
        if E is self.pe:
            deps.pop(id(E.sem), None)
        self._emit_waits(E, E.eng, deps)
        inst = fn(E.eng)
        E.count += 1
        inst.then_inc(E.sem, 1)
        self.n_inst += 1
        self._record((id(E.sem), E.sem, E.count), reads, writes)
        return inst

    def dma(self, Q, out, in_, reads=(), writes=()):
        deps = self._deps(reads, writes)
        waiter = Q.host if Q.host is not None else Q
        i = Q.idx
        Q.idx = (Q.idx + 1) % len(Q.sems)
        sem = Q.sems[i]
        if Q.totals[i] > 0:
            deps[id(sem)] = (sem, Q.totals[i])
        self._emit_waits(waiter, Q.eng, deps)
        inst = Q.eng.dma_start(out=out, in_=in_)
        Q.totals[i] += 16
        inst.then_inc(sem, 16)
        self.n_inst += 1
        self._record((id(sem), sem, Q.totals[i]), reads, writes)
        return inst

    def wait_all(self, E, keys):
        self._emit_waits(E, E.eng, self._deps(keys, ()))


class WStream:
    def __init__(self, P, panels, nk, slots, name, depth=None):
        self.P, self.panels, self.nk, self.slots, self.name = P, panels, nk, slots, name
        self.depth = depth or len(slots)
        self.nxt = 0

    def get(self, i):
        kb = self.P.kb
        while self.nxt < min(len(self.panels), i + self.depth):
            j = self.nxt
            s = j % len(self.slots)
            kb.dma(kb.q_pool, self.slots[s][:, 0:self.nk, :], self.panels[j], writes=[(self.name, s)])
            self.nxt += 1
        s = i % len(self.slots)
        return self.slots[s], (self.name, s)


class Prog:
    def __init__(self, n_seq, layers, dbg=(), lite=False):
        nc = self.nc = bass.Bass("TRN2", target_bir_lowering=False)
        kb = self.kb = KB(nc)
        self.n_seq, self.layers, self.dbg = n_seq, layers, dbg

        def dt(name, shape, dtype=F32, kind="ExternalInput"):
            if lite and kind == "ExternalInput" and name in ("win_p", "bd", "wg2", "vecs", "wm_p", "wb_p", "wo_p", "wup_p", "wgt_p", "wdn_p", "wpg_p", "wpp_p", "pT"):
                shape = [1] + list(shape[1:])
            return nc.dram_tensor(name, list(shape), dtype, kind=kind).ap()

        self.xT = dt("xT", [n_seq, D, T])
        self.pT = dt("pT", [DEPTH, n_seq, 256, T])
        self.win = dt("win_p", [DEPTH, N_IN_MT, 128, NKT, 128])
        self.wm = dt("wm_p", [DEPTH, 3, 16, 128, 16, 128])
        self.wb = dt("wb_p", [DEPTH, 3, 16, 128, 8, 128])
        self.wo = dt("wo_p", [DEPTH, 16, 128, 16, 128])
        self.wup = dt("wup_p", [DEPTH, NFF, 128, 16, 128])
        self.wgt = dt("wgt_p", [DEPTH, NFF, 128, 16, 128])
        self.wdn = dt("wdn_p", [DEPTH, 16, 128, NFF, 128])
        self.wpg = dt("wpg_p", [DEPTH, 16, 128, 16, 128])
        self.wpp = dt("wpp_p", [DEPTH, 16, 128, 2, 128])
        self.bd = dt("bd", [DEPTH, 2, 8, 128, 128])
        self.wg2 = dt("wg2", [DEPTH, 16, 512])
        self.vecs = dt("vecs", [DEPTH, 128, NV])
        self.consts = dt("consts", [128, NCON])
        self.outT = dt("outT", [n_seq, D, T], kind="ExternalOutput")
        self.dbg_out = {n: dt("dbg_" + n, shp, dty, kind="ExternalOutput") for n, shp, dty in dbg}
        self.yscr = dt("yscr", [3, 1024, T], BF16, "Internal")
        self.gscr = dt("gscr", [3, D, T], BF16, "Internal")
        self.zscr = dt("zscr", [D, T], F32, "Internal")
        self.xres = dt("xres", [D, T], F32, "Internal")
        self.ascr = dt("ascr", [DFF, T], BF16, "Internal")
        if any(n == "merged" for n, _, _ in dbg):
            self.yin = dt("yin", [3, 1024, T], BF16)
        if any(n == "act" for n, _, _ in dbg):
            self.x1in = dt("x1in", [D, T])
        self.con = kb.sbuf("con", [128, NCON], F32)
        self.identb = kb.sbuf("identb", [128, 128], BF16)
        self.onesb = kb.sbuf("onesb", [128, 128], BF16)
        self.onesf = kb.sbuf("onesf", [128, 128], F32)
        self.vec = kb.sbuf("vec", [128, NV], F32)
        self.der = kb.sbuf("der", [128, 32], F32)
        self.XB = kb.sbuf("XB", [128, NKT, T], BF16)
        self.ws = [kb.sbuf(f"ws{i}", [128, NKT, 128], BF16) for i in range(3)]
        self.pg = [kb.psum(f"pg{i}", [128, TB]) for i in range(2)]
        self.pm = [kb.psum(f"pm{i}", [128, TB]) for i in range(5)]
        self.pt = kb.psum("pt", [128, 2 * TB], BF16)
        self.pg_i = 0
        self.out_keys = []

    def vcol(self, name, i=0, p=128):
        c = VC[name] + i
        return self.vec[0:p, c:c + 1]

    def next_pg(self):
        i = self.pg_i
        self.pg_i = (i + 1) % len(self.pg)
        return self.pg[i], ("pg", i)

    def A(self, out, in_, func, reads, writes, bias=None, scale=None):
        kw = {}
        if bias is not None:
            kw["bias"] = bias
        if scale is not None:
            kw["scale"] = scale
        return self.kb.op(self.kb.act, lambda e: e.activation(out=out, in_=in_, func=func, **kw), reads, writes)

    def TT(self, out, a, b, op, reads, writes, E=None):
        return self.kb.op(E or self.kb.dve, lambda e: e.tensor_tensor(out=out, in0=a, in1=b, op=op), reads, writes)

    def TS(self, out, a, s1, op0, reads, writes, s2=None, op1=None, E=None):
        if op1 is None:
            return self.kb.op(E or self.kb.dve, lambda e: e.tensor_scalar(out=out, in0=a, scalar1=s1, scalar2=None, op0=op0), reads, writes)
        return self.kb.op(E or self.kb.dve, lambda e: e.tensor_scalar(out=out, in0=a, scalar1=s1, scalar2=s2, op0=op0, op1=op1), reads, writes)

    def STT(self, out, a, s, b, op0, op1, reads, writes):
        return self.kb.op(self.kb.dve, lambda e: e.scalar_tensor_tensor(out=out, in0=a, scalar=s, in1=b, op0=op0, op1=op1), reads, writes)

    def MM(self, out, lhsT, rhs, start, stop, reads, writes):
        return self.kb.op(self.kb.pe, lambda e: e.matmul(out, lhsT=lhsT, rhs=rhs, start=start, stop=stop), reads, writes)

    def CP(self, out, in_, reads, writes, E=None):
        return self.kb.op(E or self.kb.dve, lambda e: e.tensor_copy(out=out, in_=in_), reads, writes)

    def barrier(self):
        kb = self.kb
        engs = [kb.pe, kb.act, kb.dve, kb.pool]
        for E in engs + [kb.q_sp]:
            eng = E.eng
            deps = {}
            for E2 in engs:
                if E2 is not E and E2.count > 0:
                    deps[id(E2.sem)] = (E2.sem, E2.count)
            for Q in (kb.q_sp, kb.q_pool):
                for sem, tot in zip(Q.sems, Q.totals):
                    if tot > 0:
                        deps[id(sem)] = (sem, tot)
            kb._emit_waits(E, eng, deps)

    def dump(self, name, src_ap, dst_slice, reads):
        if name in self.dbg_out:
            self.kb.dma(self.kb.q_sp, dst_slice(self.dbg_out[name]), src_ap, reads=reads, writes=[("dbg", name)])
            self.out_keys.append(("dbg", name))

    def setup(self):
        kb = self.kb
        kb.dma(kb.q_sp, self.con[:], self.consts, writes=["con"])
        self.CP(self.identb[:], self.con[:, C_ID:C_ID + 128], ["con"], ["identb"])
        kb.op(kb.dve, lambda e: e.memset(self.onesb[:], 1.0), (), ["onesb"])
        kb.op(kb.dve, lambda e: e.memset(self.onesf[:], 1.0), (), ["onesf"])

    def layer_setup(self, l):
        kb = self.kb
        kb.dma(kb.q_sp, self.vec[:], self.vecs[l], writes=["vec"])
        d = self.der
        lam = self.vec[:, VC["lru_lam"]:VC["lru_lam"] + 8]
        self.A(d[:, 0:8], lam, AF.Exp, ["vec"], ["der"], scale=-1.0)
        self.A(d[:, 0:8], d[:, 0:8], AF.Ln, ["der"], ["der"], bias=1.0)
        self.TS(d[:, 8:16], d[:, 0:8], -16.0, ALU.mult, ["der"], ["der"])
        self.TS(d[:, 0:8], d[:, 0:8], -8.0, ALU.mult, ["der"], ["der"])
        self.TS(d[:, 16:20], self.vec[:, VC["gla_bg2"]:VC["gla_bg2"] + 4], -1.0, ALU.mult, ["vec", "der"], ["der"])
        self.A(d[0:8, 20:21], self.vcol("gdn_alog", 0, 8), AF.Exp, ["vec", "der"], ["der"])
        self.TS(d[0:8, 20:21], d[0:8, 20:21], -1.0, ALU.mult, ["der"], ["der"])

    def load_x0(self, s, src=None):
        kb = self.kb
        src = self.xT[s] if src is None else src
        for ft in range(NKT):
            kb.dma(kb.q_pool, self.XB[:, ft, :], src[ft * 128:(ft + 1) * 128, :], writes=[("XB", ft)])

    def xb_keys(self, k):
        return [("XB", k)]

    def proj_xb(self, wslot, wkey, M, epi, nk=NKT):
        for tb in range(NTB):
            ps, pkey = self.next_pg()
            for k in range(nk):
                self.MM(ps[0:M, :], wslot[:, k, 0:M], self.XB[:, k, tb * TB:(tb + 1) * TB], k == 0, k == nk - 1,
                        [wkey, ("XB", k)], [pkey])
            epi(tb, ps, pkey)

    def phase_alloc(self):
        es = contextlib.ExitStack()
        nc = self.nc
        self._uid = getattr(self, "_uid", 0) + 1
        u = self._uid

        def sb(name, shape, dtype=F32):
            return es.enter_context(nc.sbuf_tensor(f"{name}_{u}", list(shape), dtype))

        return es, sb

    def phase_lru(self, l):
        kb = self.kb
        self.barrier()
        es, sb = self.phase_alloc()
        with es:
            lxp = [sb(f"lxp{i}", [128, T + 4]) for i in range(2)]
            gt = [sb(f"gt{i}", [128, T]) for i in range(2)]
            xa, r, ii, a, hh = [sb(n, [128, T]) for n in ("xa", "r", "ii", "a", "hh")]
            xab = sb("xab", [128, T], BF16)
            yt = [sb(f"yt{i}", [128, T], BF16) for i in range(2)]
            bdw = [sb(f"bdw{i}", [128, 2, 128], BF16) for i in range(2)]
            for i in range(2):
                kb.op(kb.dve, lambda e: e.memset(lxp[i][:, 0:4], 0.0), (), [("lxp", i, "h")])
            panels = []
            for j in range(8):
                panels += [self.win[l, MT_LRUX + j], self.win[l, MT_LRUG + j]]
            wst = WStream(self, panels, NKT, self.ws, "ws")
            for j in range(8):
                s = j % 2
                kb.dma(kb.q_pool, bdw[s][:, :, :], self.bd[l, :, j].rearrange("g p m -> p g m"), writes=[("bdw", s)])
                w0, k0 = wst.get(2 * j)

                def epi_x(tb, ps, pkey):
                    self.A(lxp[s][:, 4 + tb * TB:4 + (tb + 1) * TB], ps[:], AF.Copy, [pkey], [("lxp", s, tb)])
                self.proj_xb(w0, k0, 128, epi_x)
                self.bgs(24)
                w1, k1 = wst.get(2 * j + 1)

                def epi_g(tb, ps, pkey):
                    self.A(gt[s][:, tb * TB:(tb + 1) * TB], ps[:], AF.Copy, [pkey], [("gt", s, tb)])
                self.proj_xb(w1, k1, 128, epi_g)
                self.bgs(24)
                lk = [("lxp", s, tb) for tb in range(NTB)] + [("lxp", s, "h")]
                gk = [("gt", s, tb) for tb in range(NTB)]
                cw = lambda tap: self.vcol("lru_cw", j * 4 + tap)
                self.TS(xa[:], lxp[s][:, 4:4 + T], cw(3), ALU.mult, lk + ["vec"], ["xa"], s2=self.vcol("lru_cb", j), op1=ALU.add)
                for tap in (2, 1, 0):
                    self.STT(xa[:], lxp[s][:, 1 + tap:1 + tap + T], cw(tap), xa[:], ALU.mult, ALU.add, lk + ["xa", "vec"], ["xa"])
                self.CP(xab[:], xa[:], ["xa"], ["xab"], E=kb.pool)
                for gi, (dst, bname) in enumerate(((r, "lru_br"), (ii, "lru_bi"))):
                    for tb in range(NTB):
                        ps, pkey = self.next_pg()
                        self.MM(ps[:], bdw[s][:, gi, :], xab[:, tb * TB:(tb + 1) * TB], True, True, [("bdw", s), "xab"], [pkey])
                        self.A(dst[:, tb * TB:(tb + 1) * TB], ps[:], AF.Sigmoid, [pkey, "vec"], [(bname, tb)], bias=self.vcol(bname, j))
                    self.bgs(24)
                rk = [("lru_br", tb) for tb in range(NTB)]
                ik = [("lru_bi", tb) for tb in range(NTB)]
                self.A(a[:], r[:], AF.Exp, rk + ["der"], ["a"], scale=self.der[:, j:j + 1])
                self.A(r[:], r[:], AF.Exp, rk + ["der"], rk, scale=self.der[:, 8 + j:9 + j])
                self.TS(r[:], r[:], -1.0, ALU.mult, rk, rk, s2=1.0, op1=ALU.add)
                self.A(r[:], r[:], AF.Sqrt, rk, rk)
                self.TT(ii[:], ii[:], xa[:], ALU.mult, ik + ["xa"], ik)
                self.TT(ii[:], ii[:], r[:], ALU.mult, ik + rk, ik)
                kb.op(kb.dve, lambda e: e.tensor_tensor_scan(out=hh[:], data0=a[:], data1=ii[:], initial=0.0, op0=ALU.mult, op1=ALU.add),
                      ["a"] + ik, ["hh"])
                self.A(xa[:], gt[s][:], AF.Square, gk + ["xa"], ["xa"])
                self.TS(xa[:], xa[:], 0.044715, ALU.mult, ["xa"], ["xa"], s2=1.0, op1=ALU.add)
                self.TT(xa[:], xa[:], gt[s][:], ALU.mult, ["xa"] + gk, ["xa"])
                self.A(xa[:], xa[:], AF.Sigmoid, ["xa"], ["xa"], scale=1.5957691216057308)
                self.TT(xa[:], xa[:], gt[s][:], ALU.mult, ["xa"] + gk, ["xa"])
                self.TT(yt[s][:], hh[:], xa[:], ALU.mult, ["hh", "xa"], [("yt", s)])
                kb.dma(kb.q_sp, self.yscr[0, j * 128:(j + 1) * 128, :], yt[s][:], reads=[("yt", s)], writes=[("yscr", 0, j)])
                self.dump("ya", yt[s][:], lambda o: o[j * 128:(j + 1) * 128, :], [("yt", s)])
            self.bg_close_group()
            self.barrier()

    def PT(self, out, in_, reads, writes):
        return self.kb.op(self.kb.pe, lambda e: e.transpose(out=out, in_=in_, identity=self.identb[:]), reads + ["identb"], writes)

    def phase_gla(self, l):
        kb = self.kb
        self.barrier()
        es, sb = self.phase_alloc()
        with es:
            glr16 = sb("glr16", [16, T], BF16)
            wg2b = sb("wg2b", [16, 512], BF16)
            b, eb, enb = [sb(n, [128, T]) for n in ("b", "eb", "enb")]
            ebl = sb("ebl", [128, 32])
            qd, ki, kd = [sb(n, [128, T], BF16) for n in ("qd", "ki", "kd")]
            vtm = sb("vtm", [128, 16, 256], BF16)
            sg = sb("sg", [128, 2, T], BF16)
            O = sb("O", [128, 2, T])
            S = sb("S", [128, 256])
            Sb = [sb(f"Sb{i}", [128, 256], BF16) for i in range(3)]
            scT = [sb(f"scT{i}", [128, 128], BF16) for i in range(2)]
            kdT = [sb(f"kdT{i}", [128, 128], BF16) for i in range(2)]
            wv = sb("wv", [128, NKT, 256], BF16)
            sq = sb("sq", [128, 2, TB], BF16)
            rs = sb("rs", [128, TB])
            t2 = sb("t2", [128, TB])
            yt = [sb(f"yt{i}", [128, T], BF16) for i in range(2)]
            pmA, pmS, pmO = self.pm[0], self.pm[1], self.pm[2]
            kb.dma(kb.q_pool, wg2b[:], self.wg2[l], writes=["wg2b"])
            panels = [self.win[l, MT_GLR]]
            for h in range(4):
                panels += [self.win[l, MT_GQ + h], self.win[l, MT_GK + h], self.win[l, MT_GOG + 2 * h], self.win[l, MT_GOG + 2 * h + 1]]
            wst = WStream(self, panels, NKT, self.ws, "ws")
            w0, k0 = wst.get(0)

            def epi_glr(tb, ps, pkey):
                self.A(glr16[:, tb * TB:(tb + 1) * TB], ps[0:16, :], AF.Copy, [pkey], [("glr", tb)])
            self.proj_xb(w0, k0, 16, epi_glr)
            yi = 0
            for h in range(4):
                tbs = lambda tb: slice(tb * TB, (tb + 1) * TB)
                for tb in range(NTB):
                    ps, pkey = self.next_pg()
                    self.MM(ps[:], wg2b[:, h * 128:(h + 1) * 128], glr16[:, tbs(tb)], True, True, ["wg2b", ("glr", tb)], [pkey])
                    self.A(enb[:, tbs(tb)], ps[:], AF.Exp, [pkey, "der"], [("enb", tb)], scale=-1.0, bias=self.der[:, 16 + h:17 + h])
                    self.A(enb[:, tbs(tb)], enb[:, tbs(tb)], AF.Ln, [("enb", tb)], [("enb", tb)], bias=1.0)
                    self.TS(enb[:, tbs(tb)], enb[:, tbs(tb)], -1.0 / 16.0, ALU.mult, [("enb", tb)], [("enb", tb)])
                ek = [("enb", tb) for tb in range(NTB)]
                kb.op(kb.dve, lambda e: e.tensor_tensor_scan(out=b[:], data0=self.con[:, C_CM:C_CM + T], data1=enb[:], initial=0.0,
                                                             op0=ALU.mult, op1=ALU.add), ek + ["con"], ["b"])
                self.A(eb[:], b[:], AF.Exp, ["b"], ["eb"])
                self.A(enb[:], b[:], AF.Exp, ["b"] + ek, ek, scale=-1.0)
                self.A(ebl[:], b[:, 63:T:64], AF.Exp, ["b"], ["ebl"])
                wq, kq = wst.get(1 + 4 * h)

                def epi_q(tb, ps, pkey):
                    self.STT(qd[:, tbs(tb)], ps[:], 128 ** -0.5, eb[:, tbs(tb)], ALU.mult, ALU.mult, [pkey, "eb"], [("qd", tb)])
                self.proj_xb(wq, kq, 128, epi_q)
                self.bgs(32)
                wk, kk = wst.get(2 + 4 * h)

                def epi_k(tb, ps, pkey):
                    self.TT(ki[:, tbs(tb)], ps[:], enb[:, tbs(tb)], ALU.mult, [pkey] + ek, [("ki", tb)])
                    self.TT(kd[:, tbs(tb)].rearrange("p (c t) -> p c t", t=64), ki[:, tbs(tb)].rearrange("p (c t) -> p c t", t=64),
                            ebl[:, tb * 8:(tb + 1) * 8].unsqueeze(2).to_broadcast([128, 8, 64]), ALU.mult,
                            [("ki", tb), "ebl"], [("kd", tb)], E=kb.pool)
                self.proj_xb(wk, kk, 128, epi_k)
                self.bgs(32)
                for half in range(2):
                    kb.dma(kb.q_pool, wv[:, :, half * 128:(half + 1) * 128], self.win[l, MT_GV + 2 * h + half], writes=[("wv", half)])
                for tt in range(16):
                    ps, pkey = self.next_pg()
                    for k in range(NKT):
                        self.MM(ps[:, 0:256], self.XB[:, k, tt * 128:(tt + 1) * 128], wv[:, k, :], k == 0, k == NKT - 1,
                                [("wv", 0), ("wv", 1), ("XB", k)], [pkey])
                    self.A(vtm[:, tt, :], ps[:, 0:256], AF.Copy, [pkey], [("vtm", tt)])
                    self.bgs(4)
                for vh in range(2):
                    wg, kg = wst.get(3 + 4 * h + vh)

                    def epi_g(tb, ps, pkey):
                        self.A(sg[:, vh, tbs(tb)], ps[:], AF.Silu, [pkey], [("sg", vh, tb)])
                    self.proj_xb(wg, kg, 128, epi_g)
                kb.op(kb.dve, lambda e: e.memset(S[:], 0.0), (), ["S"])
                kb.op(kb.dve, lambda e: e.memset(Sb[0][:], 0.0), (), [("Sb", 0)])
                si = 0
                for pr in range(16):
                    tb = pr // 4
                    cols = slice(pr * 128, (pr + 1) * 128)
                    x = pr % 2
                    self.MM(pmA[:, 0:128], ki[:, cols], qd[:, cols], True, True, [("ki", tb), ("qd", tb)], ["pmA"])
                    self.TT(scT[x][:], pmA[:, 0:128], self.con[:, C_MIT:C_MIT + 128], ALU.mult, ["pmA", "con"], [("scT", x)])
                    self.PT(self.pt[:, 0:128], kd[:, cols], [("kd", tb)], ["pt"])
                    self.A(kdT[x][:], self.pt[:, 0:128], AF.Copy, ["pt"], [("kdT", x)])
                    self.bgs(8)
                    sbi = [si]
                    for c in range(2):
                        pp = slice(c * 64, (c + 1) * 64)
                        self.MM(pmS[:, 0:256], kdT[x][pp, :], vtm[pp, pr, :], True, True, [("kdT", x), ("vtm", pr)], ["pmS"])
                        ci = pr * 2 + c
                        self.STT(S[:], S[:], ebl[:, ci:ci + 1], pmS[:, 0:256], ALU.mult, ALU.add, ["S", "ebl", "pmS"], ["S"])
                        si = (si + 1) % 3
                        self.A(Sb[si][:], S[:], AF.Copy, ["S"], [("Sb", si)])
                        sbi.append(si)
                        self.bgs(8)
                        if c == 0:
                            for vh in range(2):
                                vs = slice(vh * 128, (vh + 1) * 128)
                                self.MM(pmO[:, vs], vtm[:, pr, vs], scT[x][:], True, False, [("vtm", pr), ("scT", x)], ["pmO"])
                                for cc in range(2):
                                    self.MM(pmO[:, vh * 128 + cc * 64:vh * 128 + (cc + 1) * 64], Sb[sbi[cc]][:, vs],
                                            qd[:, pr * 128 + cc * 64:pr * 128 + (cc + 1) * 64], False, cc == 1,
                                            [("Sb", sbi[cc]), ("qd", tb)], ["pmO"])
                            self.A(O[:, :, cols], pmO[:, 0:256].rearrange("p (a b) -> p a b", a=2), AF.Copy, ["pmO"], [("O", pr)])
                for tb in range(NTB):
                    ok = [("O", pr) for pr in range(tb * 4, tb * 4 + 4)]
                    self.A(sq[:], O[:, :, tbs(tb)], AF.Square, ok, ["sq"])
                    ps, pkey = self.next_pg()
                    for vh in range(2):
                        self.MM(ps[:], self.onesb[:], sq[:, vh, :], vh == 0, vh == 1, ["onesb", "sq"], [pkey])
                    self.A(rs[:], ps[:], AF.Ln, [pkey], ["rs"], scale=1.0 / 256.0, bias=NORM_EPS)
                    self.A(rs[:], rs[:], AF.Exp, ["rs"], ["rs"], scale=-0.5)
                    for vh in range(2):
                        self.STT(t2[:], O[:, vh, tbs(tb)], self.vcol("gla_ng", vh), rs[:], ALU.mult, ALU.mult, ok + ["vec", "rs"], ["t2"])
                        self.TT(yt[vh][:, tbs(tb)], t2[:], sg[:, vh, tbs(tb)], ALU.mult, ["t2", ("sg", vh, tb)], [("ytb", vh, tb)])
                for vh in range(2):
                    f = h * 2 + vh
                    yk = [("ytb", vh, tb) for tb in range(NTB)]
                    kb.dma(kb.q_sp, self.yscr[1, f * 128:(f + 1) * 128, :], yt[vh][:], reads=yk, writes=[("yscr", 1, f)])
                    self.dump("yb", yt[vh][:], lambda o: o[f * 128:(f + 1) * 128, :], yk)
                    self.dump("ob", O[:, vh, :], lambda o: o[f * 128:(f + 1) * 128, :], [("O", pr) for pr in range(16)])
            self.bg_drain()
            self.barrier()

    def gates_bg(self, l, sb):
        gt = [sb(f"bggt{i}", [128, TB], BF16) for i in range(2)]
        wsl = [sb(f"bgws{i}", [128, NKT, 128], BF16) for i in range(2)]
        return self._gates_bg(l, gt, wsl)

    def _gates_bg(self, l, gt, wsl):
        kb = self.kb
        panels = [self.wm[l, n, mt] for n in range(3) for mt in range(16)]
        wst = WStream(self, panels, NKT, wsl, "bgws")
        gi = 0
        for i in range(48):
            n, mt = divmod(i, 16)
            w, k = wst.get(i)
            for tb in range(NTB):
                bi = self.bg_bank_i
                self.bg_bank_i ^= 1
                bank_idx = self.bg_banks[bi]
                ps, pkey = self.pm[bank_idx], ("pm", bank_idx)
                self.bg_open = True
                for kk in range(NKT):
                    self.MM(ps[:], w[:, kk, :], self.XB[:, kk, tb * TB:(tb + 1) * TB], kk == 0, kk == NKT - 1, [k, ("XB", kk)], [pkey])
                    if kk < NKT - 1:
                        yield
                g = gi % 2
                gi += 1
                self.A(gt[g][:], ps[:], AF.Sigmoid, ["vec"], [("bggt", g), pkey], bias=self.vcol("b_merge", n * 16 + mt))
                kb.dma(kb.q_sp, self.gscr[n, mt * 128:(mt + 1) * 128, tb * TB:(tb + 1) * TB], gt[g][:], reads=[("bggt", g)], writes=[("gscr", n, mt)])
                self.bg_open = False
                yield

    def bgs(self, n):
        g = getattr(self, "bg", None)
        if g is None:
            return
        for _ in range(n):
            try:
                next(g)
            except StopIteration:
                self.bg = None
                return

    def bg_close_group(self):
        while getattr(self, "bg", None) is not None and self.bg_open:
            self.bgs(1)

    def bg_drain(self):
        while getattr(self, "bg", None) is not None:
            self.bgs(64)

    def phase_gates(self, l):
        kb = self.kb
        self.barrier()
        es, sb = self.phase_alloc()
        with es:
            gtile = [sb(f"gtile{i}", [128, T], BF16) for i in range(2)]
            panels = [self.wm[l, n, mt] for n in range(3) for mt in range(16)]
            wst = WStream(self, panels, NKT, self.ws, "ws")
            for i in range(48):
                n, mt = divmod(i, 16)
                s = i % 2
                w, k = wst.get(i)

                def epi(tb, ps, pkey):
                    self.A(gtile[s][:, tb * TB:(tb + 1) * TB], ps[:], AF.Sigmoid, [pkey, "vec"], [("gtile", s, tb)],
                           bias=self.vcol("b_merge", n * 16 + mt))
                self.proj_xb(w, k, 128, epi)
                kb.dma(kb.q_sp, self.gscr[n, mt * 128:(mt + 1) * 128, :], gtile[s][:],
                       reads=[("gtile", s, tb) for tb in range(NTB)], writes=[("gscr", n, mt)])
            self.barrier()

    def x_src(self, l, s):
        return self.xT[s] if l == self.layers[0] else self.xres

    def phase_merge(self, l, s):
        kb = self.kb
        self.barrier()
        es, sb = self.phase_alloc()
        with es:
            Y = [sb(f"Y{i}", [128, 3, 8, TB], BF16) for i in range(2)]
            gl = [sb(f"gl{i}", [128, 3, TB], BF16) for i in range(3)]
            wbs = [sb(f"wbs{i}", [128, 8, 128], BF16) for i in range(6)]
            m0 = sb("m0", [128, TB])
            m1 = sb("m1", [128, TB])
            panels = [self.wb[l, n, mt] for _tb in range(NTB) for mt in range(16) for n in range(3)]
            wst = WStream(self, panels, 8, wbs, "wbs")
            ysrc = self.yin if getattr(self, "fake_y", False) else self.yscr
            pi = 0
            for tb in range(NTB):
                ys = tb % 2
                for n in range(3):
                    kb.dma(kb.q_sp, Y[ys][:, n, :, :], ysrc[n, :, tb * TB:(tb + 1) * TB].rearrange("(f p) t -> p f t", p=128),
                           reads=[("yscr", n, f) for f in range(8)], writes=[("Y", ys, n)])
                for mt in range(16):
                    g = (tb * 16 + mt) % 3
                    kb.dma(kb.q_sp, gl[g][:], self.gscr[:, mt * 128:(mt + 1) * 128, tb * TB:(tb + 1) * TB].rearrange("n p t -> p n t"),
                           reads=[("gscr", n, mt) for n in range(3)], writes=[("gl", g)])
                    for n in range(3):
                        w, k = wst.get(pi)
                        pi += 1
                        ps = self.pm[n]
                        for kk in range(8):
                            self.MM(ps[:], w[:, kk, :], Y[ys][:, n, kk, :], kk == 0, kk == 7, [k, ("Y", ys, n)], [("pm", n)])
                    self.TT(m0[:], self.pm[0][:], gl[g][:, 0, :], ALU.mult, [("pm", 0), ("gl", g)], ["m0"])
                    self.TT(m1[:], self.pm[1][:], gl[g][:, 1, :], ALU.mult, [("pm", 1), ("gl", g)], ["m1"])
                    self.TT(m0[:], m0[:], m1[:], ALU.add, ["m0", "m1"], ["m0"])
                    self.TT(m1[:], self.pm[2][:], gl[g][:, 2, :], ALU.mult, [("pm", 2), ("gl", g)], ["m1"])
                    self.TT(self.XB[:, mt, tb * TB:(tb + 1) * TB], m0[:], m1[:], ALU.add, ["m0", "m1"], [("XB", mt)], E=kb.pool)
            for mt in range(16):
                self.dump("merged", self.XB[:, mt, :], lambda o: o[mt * 128:(mt + 1) * 128, :], [("XB", mt)])
            xt = [sb(f"xt{i}", [128, TB]) for i in range(3)]
            zt = [sb(f"zt{i}", [128, TB]) for i in range(3)]
            xsrc = self.x_src(l, s)
            wst = WStream(self, [self.wo[l, mt] for mt in range(16)], NKT, self.ws, "ws")
            cnt = [0]
            for mt in range(16):
                w, k = wst.get(mt)

                def epi(tb, ps, pkey):
                    i = cnt[0] % 3
                    cnt[0] += 1
                    kb.dma(kb.q_sp, xt[i][:], xsrc[mt * 128:(mt + 1) * 128, tb * TB:(tb + 1) * TB], reads=[("xres", mt, tb)], writes=[("xt", i)])
                    self.STT(zt[i][:], xt[i][:], ALPHA, ps[:], ALU.mult, ALU.add, [("xt", i), pkey], [("zt", i)])
                    kb.dma(kb.q_sp, self.zscr[mt * 128:(mt + 1) * 128, tb * TB:(tb + 1) * TB], zt[i][:], reads=[("zt", i)], writes=[("zscr", mt, tb)])
                self.proj_xb(w, k, 128, epi)
            self.barrier()
        self.ln_pass(l, s, "ln1_g", "ln1_b", "x1", final=False)

    def ln_pass(self, l, s, gname, bname, dbgname, final):
        kb = self.kb
        self.barrier()
        es, sb = self.phase_alloc()
        with es:
            Zs = [sb(f"Z{i}", [128, NKT, TB]) for i in range(2)]
            sqt = [sb(f"sqt{i}", [128, TB]) for i in range(2)]
            mu = sb("mu", [128, TB])
            rstd = sb("rstd", [128, TB])
            ot = [sb(f"ot{i}", [128, TB]) for i in range(3)]
            pa, pb = self.pm[0], self.pm[1]
            oi = 0
            for tb in range(NTB):
                tsl = slice(tb * TB, (tb + 1) * TB)
                Z = Zs[tb % 2]
                zi = tb % 2
                for ft in range(NKT):
                    kb.dma(kb.q_sp, Z[:, ft, :], self.zscr[ft * 128:(ft + 1) * 128, tsl], reads=[("zscr", ft, tb)], writes=[("Z", zi, ft)])
                for ft in range(NKT):
                    self.MM(pa[:], self.onesf[:], Z[:, ft, :], ft == 0, ft == NKT - 1, ["onesf", ("Z", zi, ft)], ["pa"])
                self.A(mu[:], pa[:], AF.Copy, ["pa"], ["mu"], scale=1.0 / D)
                for ft in range(NKT):
                    self.TT(Z[:, ft, :], Z[:, ft, :], mu[:], ALU.subtract, [("Z", zi, ft), "mu"], [("Z", zi, ft)])
                    q = ft % 2
                    self.A(sqt[q][:], Z[:, ft, :], AF.Square, [("Z", zi, ft)], [("sqt", q)])
                    self.MM(pb[:], self.onesf[:], sqt[q][:], ft == 0, ft == NKT - 1, ["onesf", ("sqt", q)], ["pb"])
                self.A(rstd[:], pb[:], AF.Ln, ["pb"], ["rstd"], scale=1.0 / D, bias=LN_EPS)
                self.A(rstd[:], rstd[:], AF.Exp, ["rstd"], ["rstd"], scale=-0.5)
                for ft in range(NKT):
                    i = oi % 3
                    oi += 1
                    self.TT(Z[:, ft, :], Z[:, ft, :], rstd[:], ALU.mult, [("Z", zi, ft), "rstd"], [("Z", zi, ft)])
                    self.TS(ot[i][:], Z[:, ft, :], self.vcol(gname, ft), ALU.mult, [("Z", zi, ft), "vec"], [("ot", i)],
                            s2=self.vcol(bname, ft), op1=ALU.add, E=kb.pool)
                    self.A(self.XB[:, ft, tsl], ot[i][:], AF.Copy, [("ot", i)], [("XB", ft)])
                    rows = slice(ft * 128, (ft + 1) * 128)
                    if final:
                        kb.dma(kb.q_sp, self.outT[s, rows, tsl], ot[i][:], reads=[("ot", i)], writes=[("outT", s, ft, tb)])
                        self.out_keys.append(("outT", s, ft, tb))
                    else:
                        kb.dma(kb.q_sp, self.xres[rows, tsl], ot[i][:], reads=[("ot", i)], writes=[("xres", ft, tb)])
                    self.dump(dbgname, ot[i][:], lambda o: o[rows, tsl], [("ot", i)])
            self.barrier()

    def phase_ffn(self, l, s, final=False):
        kb = self.kb
        self.barrier()
        es, sb = self.phase_alloc()
        with es:
            gtp = [sb(f"gtp{i}", [128, T + 2]) for i in range(2)]
            up = [sb(f"up{i}", [128, T]) for i in range(2)]
            cv = sb("cv", [128, T])
            at = [sb(f"at{i}", [128, T], BF16) for i in range(2)]
            for i in range(2):
                kb.op(kb.dve, lambda e: e.memset(gtp[i][:, 0:2], 0.0), (), [("gtp", i, "h")])
            panels = []
            for f in range(NFF):
                panels += [self.wgt[l, f], self.wup[l, f]]
            wst = WStream(self, panels, NKT, self.ws, "ws")
            for f in range(NFF):
                q = f % 2
                w0, k0 = wst.get(2 * f)

                def epi_g(tb, ps, pkey):
                    self.A(gtp[q][:, 2 + tb * TB:2 + (tb + 1) * TB], ps[:], AF.Copy, [pkey], [("gtp", q, tb)])
                self.proj_xb(w0, k0, 128, epi_g)
                w1, k1 = wst.get(2 * f + 1)

                def epi_u(tb, ps, pkey):
                    self.A(up[q][:, tb * TB:(tb + 1) * TB], ps[:], AF.Copy, [pkey], [("up", q, tb)])
                self.proj_xb(w1, k1, 128, epi_u)
                gk = [("gtp", q, tb) for tb in range(NTB)] + [("gtp", q, "h")]
                uk = [("up", q, tb) for tb in range(NTB)]
                cw = lambda tap: self.vcol("ffn_cw", f * 3 + tap)
                self.TS(cv[:], gtp[q][:, 2:2 + T], cw(2), ALU.mult, gk + ["vec"], ["cv"], s2=self.vcol("ffn_cb", f), op1=ALU.add)
                for tap in (1, 0):
                    self.STT(cv[:], gtp[q][:, tap:tap + T], cw(tap), cv[:], ALU.mult, ALU.add, gk + ["cv", "vec"], ["cv"])
                self.A(cv[:], cv[:], AF.Silu, ["cv"], ["cv"])
                self.TT(at[q][:], cv[:], up[q][:], ALU.mult, ["cv"] + uk, [("at", q)], E=kb.pool)
                kb.dma(kb.q_sp, self.ascr[f * 128:(f + 1) * 128, :], at[q][:], reads=[("at", q)], writes=[("ascr", f)])
                self.dump("act", at[q][:], lambda o: o[f * 128:(f + 1) * 128, :], [("at", q)])
            self.barrier()
        es, sb = self.phase_alloc()
        with es:
            pTb = sb("pTb", [128, 2, T], BF16)
            sgt = [sb(f"sgt{i}", [128, T]) for i in range(2)]
            plt = [sb(f"plt{i}", [128, T]) for i in range(2)]
            wpps = [sb(f"wpps{i}", [128, 2, 128], BF16) for i in range(2)]
            for f in range(2):
                kb.dma(kb.q_pool, pTb[:, f, :], self.pT[l, s, f * 128:(f + 1) * 128, :], writes=[("pTb", f)])
            wst = WStream(self, [self.wpg[l, mt] for mt in range(16)], NKT, self.ws, "ws")
            wst2 = WStream(self, [self.wpp[l, mt] for mt in range(16)], 2, wpps, "wpps")
            for mt in range(16):
                q = mt % 2
                w, k = wst.get(mt)

                def epi_s(tb, ps, pkey):
                    self.A(sgt[q][:, tb * TB:(tb + 1) * TB], ps[:], AF.Sigmoid, [pkey], [("sgt", q, tb)])
                self.proj_xb(w, k, 128, epi_s)
                w2, k2 = wst2.get(mt)
                for tb in range(NTB):
                    ps, pkey = self.next_pg()
                    for kk in range(2):
                        self.MM(ps[:], w2[:, kk, :], pTb[:, kk, tb * TB:(tb + 1) * TB], kk == 0, kk == 1, [k2, ("pTb", kk)], [pkey])
                    self.TT(plt[q][:, tb * TB:(tb + 1) * TB], ps[:], sgt[q][:, tb * TB:(tb + 1) * TB], ALU.mult,
                            [pkey, ("sgt", q, tb)], [("plt", q, tb)])
                kb.dma(kb.q_sp, self.zscr[mt * 128:(mt + 1) * 128, :], plt[q][:], reads=[("plt", q, tb) for tb in range(NTB)],
                       writes=[("zscr", mt, tb) for tb in range(NTB)])
            self.barrier()
        es, sb = self.phase_alloc()
        with es:
            TH = 2 * TB
            AT = sb("AT", [128, NFF, TH], BF16)
            wds = [sb(f"wds{i}", [128, NFF, 128], BF16) for i in range(2)]
            xt = [sb(f"xt{i}", [128, TB]) for i in range(2)]
            pl = [sb(f"pl{i}", [128, TB]) for i in range(2)]
            wst = WStream(self, [self.wdn[l, mt] for _h in range(2) for mt in range(16)], NFF, wds, "wds")
            xsrc = self.x1in if getattr(self, "fake_x1", False) else self.xres
            cnt = 0
            for hf in range(2):
                hsl = slice(hf * TH, (hf + 1) * TH)
                for f0 in range(0, NFF, 11):
                    kb.dma(kb.q_sp, AT[:, f0:f0 + 11, :], self.ascr[f0 * 128:(f0 + 11) * 128, hsl].rearrange("(f p) t -> p f t", p=128),
                           reads=[("ascr", f) for f in range(f0, f0 + 11)], writes=[("AT", f0)])
                for mt in range(16):
                    rows = slice(mt * 128, (mt + 1) * 128)
                    w, k = wst.get(hf * 16 + mt)
                    for t2 in range(2):
                        tb = hf * 2 + t2
                        tsl = slice(tb * TB, (tb + 1) * TB)
                        i = cnt % 2
                        cnt += 1
                        ps, pkey = self.next_pg()
                        for kk in range(NFF):
                            self.MM(ps[:], w[:, kk, :], AT[:, kk, t2 * TB:(t2 + 1) * TB], kk == 0, kk == NFF - 1, [k, ("AT", (kk // 11) * 11)], [pkey])
                        kb.dma(kb.q_sp, xt[i][:], xsrc[rows, tsl], reads=[("xres", mt, tb)], writes=[("xt", i)])
                        kb.dma(kb.q_sp, pl[i][:], self.zscr[rows, tsl], reads=[("zscr", mt, tb)], writes=[("pl", i)])
                        self.STT(xt[i][:], xt[i][:], ALPHA, ps[:], ALU.mult, ALU.add, [("xt", i), pkey], [("xt", i)])
                        self.TT(xt[i][:], xt[i][:], pl[i][:], ALU.add, [("xt", i), ("pl", i)], [("xt", i)], E=kb.pool)
                        kb.dma(kb.q_sp, self.zscr[rows, tsl], xt[i][:], reads=[("xt", i)], writes=[("zscr", mt, tb)])
            self.barrier()
        self.ln_pass(l, s, "ln2_g", "ln2_b", "x2", final=final)

    def phase_gdn(self, l):
        kb = self.kb
        self.barrier()
        es, sb = self.phase_alloc()
        with es:
            gam8 = sb("gam8", [8, T])
            TM = sb("TM", [128, 16, 4, 8])
            Esel = sb("Esel", [8, 8, 128])
            negm = sb("negm", [128, 128])
            pmA, pmG, pmU, pmW, pmO = self.pm
            for h in range(8):
                self.CP(Esel[:, h, :], self.con[0:8, C_ID + h:C_ID + h + 1].to_broadcast([8, 128]), ["con"], [("Esel", h)])
            self.TS(negm[:], self.con[:, C_MS:C_MS + 128], -1.0, ALU.mult, ["con"], ["negm"])
            es2, sb2 = self.phase_alloc()
            with es2:
                da8, bt8, c1_8, c2_8, eg8 = [sb2(n, [8, T]) for n in ("da8", "bt8", "c1_8", "c2_8", "eg8")]
                wst = WStream(self, [self.win[l, MT_DAB], self.win[l, MT_DB]], NKT, self.ws, "ws")
                w0, k0 = wst.get(0)
                tbs = lambda tb: slice(tb * TB, (tb + 1) * TB)

                def epi_a(tb, ps, pkey):
                    self.A(da8[:, tbs(tb)], ps[0:8, :], AF.Exp, [pkey, "vec"], [("da8", tb)], bias=self.vcol("gdn_dtb", 0, 8))
                    self.A(da8[:, tbs(tb)], da8[:, tbs(tb)], AF.Ln, [("da8", tb)], [("da8", tb)], bias=1.0)
                    self.TS(da8[:, tbs(tb)], da8[:, tbs(tb)], self.der[0:8, 20:21], ALU.mult, [("da8", tb), "der"], [("da8", tb)])
                self.proj_xb(w0, k0, 8, epi_a)
                w1, k1 = wst.get(1)

                def epi_b(tb, ps, pkey):
                    self.A(bt8[:, tbs(tb)], ps[0:8, :], AF.Sigmoid, [pkey], [("bt8", tb)])
                self.proj_xb(w1, k1, 8, epi_b)
                dk_ = [("da8", tb) for tb in range(NTB)]
                bk_ = [("bt8", tb) for tb in range(NTB)]
                kb.op(kb.dve, lambda e: e.tensor_tensor_scan(out=gam8[:], data0=self.con[0:8, C_CM:C_CM + T], data1=da8[:], initial=0.0,
                                                             op0=ALU.mult, op1=ALU.add), dk_ + ["con"], ["gam8"])
                self.A(eg8[:], gam8[:], AF.Exp, ["gam8"], ["eg8"])
                self.TT(c1_8[:], bt8[:], eg8[:], ALU.mult, bk_ + ["eg8"], ["c1_8"])
                self.TT(c2_8[:].rearrange("p (c t) -> p c t", t=64), gam8[:, 63:T:64].unsqueeze(2).to_broadcast([8, 32, 64]),
                        gam8[:].rearrange("p (c t) -> p c t", t=64), ALU.subtract, ["gam8"], ["c2_8"])
                self.A(c2_8[:], c2_8[:], AF.Exp, ["c2_8"], ["c2_8"])
                for tt in range(16):
                    cs = slice(tt * 128, (tt + 1) * 128)
                    for qi, (src, sk) in enumerate(((gam8, ["gam8"]), (bt8, bk_), (c1_8, ["c1_8"]), (c2_8, ["c2_8"]))):
                        kb.op(kb.pe, lambda e: e.transpose(out=pmA[:, qi * 8:(qi + 1) * 8], in_=src[:, cs], identity=self.con[0:8, C_ID:C_ID + 8]),
                              sk + ["con"], ["pmA"])
                    self.A(TM[:, tt, :, :], pmA[:, 0:32].rearrange("p (a b) -> p a b", a=4), AF.Copy, ["pmA"], [("TM", tt)])
                self.barrier()
            if getattr(self, "gdn_pre_only", False):
                return
            xp = sb("xp", [128, T + 4])
            cf = sb("cf", [128, T])
            rn = sb("rn", [128, TB])
            rn2 = sb("rn2", [128, TB])
            sqt1 = sb("sqt1", [128, TB], BF16)
            sqt2 = sb("sqt2", [128, TB], BF16)
            qnb, qdec, knb, cvb, sgd = [[sb(f"{n}{p}", [128, T], BF16) for p in range(2)] for n in ("qnb", "qdec", "knb", "cvb", "sgd")]
            lastc = [sb(f"lastc{p}", [128, 32]) for p in range(2)]
            O = sb("O", [128, T])
            G4 = 4
            KDC, QKT, WTn, USB = [sb(n, [128, 16, 128], BF16) for n in ("KDC", "QKT", "WTn", "USB")]
            TMn = sb("TMn", [128, 16, 8])
            kbg4, vbt4 = [sb(n, [128, 4, 128], BF16) for n in ("kbg4", "vbt4")]
            Yr4 = [sb(f"Yr4{i}", [128, 4, 128], BF16) for i in range(2)]
            Qr4 = [sb(f"Qr4{i}", [128, 4, 128], BF16) for i in range(2)]
            Mt4 = [sb(f"Mt4{i}", [128, 4, 128], BF16) for i in range(2)]
            ndf4, ndc4 = [sb(n, [128, 4, 128]) for n in ("ndf4", "ndc4")]
            vn = [sb(f"vn{i}", [128, 128], BF16) for i in range(2)]
            S = sb("S", [128, 128])
            Sb = [sb(f"Sb{i}", [128, 128], BF16) for i in range(4)]
            self.TS(TMn[:], TM[:, :, 0, :], -1.0, ALU.mult, [("TM", tt) for tt in range(16)], ["TMn"])
            kb.op(kb.dve, lambda e: e.memset(xp[:, 0:4], 0.0), (), [("xp", "h")])
            panels = []
            for h in range(8):
                panels += [self.win[l, MT_DQ + h], self.win[l, MT_DK + h], self.win[l, MT_DV + h], self.win[l, MT_DOG + h]]
            wst = WStream(self, panels, NKT, self.ws, "ws")
            xk = [("xp", tb) for tb in range(NTB)] + [("xp", "h")]
            NH = getattr(self, 'gdn_heads', 8)

            def S0(h):
                p = h % 2

                def proj_g(w, k, epi):
                    for tb in range(NTB):
                        ps, pkey = self.next_pg()
                        for kk in range(NKT):
                            self.MM(ps[:], w[:, kk, :], self.XB[:, kk, tbs(tb)], kk == 0, kk == NKT - 1, [k, ("XB", kk)], [pkey])
                            if kk % 4 == 3 and kk < NKT - 1:
                                yield
                        epi(tb, ps, pkey)
                        yield

                def conv_silu_g(pi, ft, out_ap, out_keys):
                    w, k = wst.get(pi)

                    def epi(tb, ps, pkey):
                        self.A(xp[:, 4 + tb * TB:4 + (tb + 1) * TB], ps[:], AF.Copy, [], [("xp", tb), pkey])
                    yield from proj_g(w, k, epi)
                    cw = lambda tap: self.vcol("gdn_cw", ft * 4 + tap)
                    self.TS(cf[:], xp[:, 4:4 + T], cw(3), ALU.mult, xk + ["vec"], ["cf"])
                    yield
                    for tap in (2, 1, 0):
                        self.STT(cf[:], xp[:, 1 + tap:1 + tap + T], cw(tap), cf[:], ALU.mult, ALU.add, xk + ["cf", "vec"], ["cf"])
                        yield
                    self.A(out_ap, cf[:], AF.Silu, ["cf"], out_keys)
                    yield

                def l2n_g(scale, dst_ap, dst_keys):
                    for tb in range(NTB):
                        self.A(sqt1[:], cf[:, tbs(tb)], AF.Square, ["cf"], ["sqt1"])
                        ps, pkey = self.next_pg()
                        self.MM(ps[:], self.onesb[:], sqt1[:], True, True, ["onesb", "sqt1"], [pkey])
                        self.A(rn[:], ps[:], AF.Ln, [], ["rn", pkey], bias=NORM_EPS)
                        self.A(rn[:], rn[:], AF.Exp, ["rn"], ["rn"], scale=-0.5)
                        self.STT(cf[:, tbs(tb)], cf[:, tbs(tb)], scale, rn[:], ALU.mult, ALU.mult, ["cf", "rn"], ["cf"])
                        yield
                    self.CP(dst_ap, cf[:], ["cf"], dst_keys)
                    yield

                yield from conv_silu_g(4 * h, h, cf[:], ["cf"])
                yield from l2n_g(128 ** -0.5, qnb[p][:], [("qnb", p)])
                for tb in range(NTB):
                    ps, pkey = self.next_pg()
                    self.MM(ps[:], Esel[:, h, :], gam8[:, tbs(tb)], True, True, [("Esel", h), "gam8"], [pkey])
                    self.A(lastc[p][:, tb * 8:(tb + 1) * 8], ps[:, 63:TB:64], AF.Exp, [], [("lastc", p), pkey])
                    self.A(rn[:], ps[:], AF.Exp, [], ["rn", pkey])
                    self.TT(qdec[p][:, tbs(tb)], cf[:, tbs(tb)], rn[:], ALU.mult, ["cf", "rn"], [("qdec", p)])
                    yield
                yield from conv_silu_g(4 * h + 1, 8 + h, cf[:], ["cf"])
                yield from l2n_g(1.0, knb[p][:], [("knb", p)])
                yield from conv_silu_g(4 * h + 2, 16 + h, cvb[p][:], [("cvb", p)])
                wg, kg = wst.get(4 * h + 3)

                def epi_g(tb, ps, pkey):
                    self.A(sgd[p][:, tbs(tb)], ps[:], AF.Silu, [], [("sgd", p, tb), pkey])
                yield from proj_g(wg, kg, epi_g)

            Xb = self.pm[0][:].bitcast(BF16)
            Yb, Zb, Wb = self.pm[1][:], self.pm[2][:], self.pm[3][:]
            kX, kY, kZ, kW = ("pm", 0), ("pm", 1), ("pm", 2), ("pm", 3)
            v3 = lambda ap: ap.rearrange("p (a b) -> p a b", b=128)
            bc = lambda ap2: ap2.unsqueeze(1).to_broadcast([128, 4, 128])
            idf = self.con[:, C_ID:C_ID + 128]
            R = range(G4)
            reg = lambda r: slice(r * 128, (r + 1) * 128)
            pU, pS = self.pm[4][:], self.pt[:].bitcast(F32)
            kU, kS = ("pm", 4), "pt"

            def rr_g(*gens):
                gens = [g for g in gens if g is not None]
                while gens:
                    for g in list(gens):
                        try:
                            next(g)
                        except StopIteration:
                            gens.remove(g)
                        yield

            def S12(h):
                p = h % 2
                knb_, qnb_, cvb_, qdec_, sgd_, lastc_ = knb[p], qnb[p], cvb[p], qdec[p], sgd[p], lastc[p]
                kK, kQ, kV, kQD = ("knb", p), ("qnb", p), ("cvb", p), ("qdec", p)

                def s1(g0):
                    gs = slice(g0, g0 + G4)
                    cs = [slice((g0 + i) * 128, (g0 + i + 1) * 128) for i in R]
                    tmk = [("TM", g0 + i) for i in R]
                    colb = lambda q: TM[:, gs, q, h].unsqueeze(2).to_broadcast([128, 4, 128])
                    gk = lambda nm: [(nm, g0 + i) for i in R]
                    for i in R:
                        self.PT(Xb[:, reg(i)], knb_[:, cs[i]], [kK], [kX])
                        self.PT(Xb[:, reg(4 + i)], cvb_[:, cs[i]], [kV], [kX])
                    for i in R:
                        self.MM(Yb[:, reg(i)], knb_[:, cs[i]], knb_[:, cs[i]], True, True, [kK], [kY])
                        self.MM(Zb[:, reg(i)], knb_[:, cs[i]], qnb_[:, cs[i]], True, True, [kK, kQ], [kZ])
                        self.MM(Wb[:, reg(i)], Esel[:, h, :], gam8[:, cs[i]], True, True, [("Esel", h), "gam8"], [kW])
                    yield
                    self.TT(kbg4[:], v3(Xb[:, 0:512]), colb(2), ALU.mult, tmk, ["kbg4", kX])
                    self.TT(KDC[:, gs, :], v3(Xb[:, 0:512]), colb(3), ALU.mult, tmk, gk("KDC") + [kX])
                    self.TT(vbt4[:], v3(Xb[:, 512:1024]), colb(1), ALU.mult, tmk, ["vbt4", kX])
                    self.TT(ndf4[:], v3(Wb), TMn[:, gs, h].unsqueeze(2).to_broadcast([128, 4, 128]), ALU.add, ["TMn"], ["ndf4", kW])
                    self.TS(ndc4[:], ndf4[:], 0.0, ALU.max, ["ndf4"], ["ndc4"])
                    self.TS(ndf4[:], ndf4[:], 0.0, ALU.min, ["ndf4"], ["ndf4"])
                    self.A(ndc4[:], ndc4[:], AF.Exp, ["ndc4"], ["ndc4"], scale=-1.0)
                    self.A(ndf4[:], ndf4[:], AF.Exp, ["ndf4"], ["ndf4"])
                    self.TT(ndc4[:], v3(Yb), ndc4[:], ALU.mult, ["ndc4"], ["ndc4", kY])
                    self.TT(ndc4[:], ndc4[:], colb(1), ALU.mult, ["ndc4"] + tmk, ["ndc4"])
                    self.TT(Yr4[0][:], ndc4[:], bc(negm[:]), ALU.mult, ["ndc4", "negm"], [("Yr4", 0)])
                    self.TT(ndf4[:], v3(Zb), ndf4[:], ALU.mult, ["ndf4"], ["ndf4", kZ])
                    self.TT(QKT[:, gs, :], ndf4[:], bc(self.con[:, C_MIT:C_MIT + 128]), ALU.mult, ["ndf4", "con"], gk("QKT"))
                    for i in R:
                        self.PT(Xb[:, reg(i)], Yr4[0][:, i, :], [("Yr4", 0)], [kX])
                    self.A(Qr4[0][:], v3(Xb[:, 0:512]), AF.Copy, [], [("Qr4", 0), kX])
                    self.TT(Mt4[0][:], Qr4[0][:], bc(idf), ALU.add, [("Qr4", 0), "con"], [("Mt4", 0)])
                    yield
                    yc = qc = mc = 0
                    for j in range(1, 6):
                        for i in R:
                            self.MM(Yb[:, reg(i)], Qr4[qc][:, i, :], Yr4[yc][:, i, :], True, True, [("Qr4", qc), ("Yr4", yc)], [kY])
                        if j <= 4:
                            for i in R:
                                self.MM(Zb[:, reg(i)], Yr4[yc][:, i, :], Qr4[qc][:, i, :], True, True, [("Qr4", qc), ("Yr4", yc)], [kZ])
                        self.A(Yr4[yc ^ 1][:], v3(Yb), AF.Copy, [], [("Yr4", yc ^ 1), kY])
                        if j <= 4:
                            self.CP(Qr4[qc ^ 1][:], v3(Zb), [], [("Qr4", qc ^ 1), kZ])
                        yc ^= 1
                        if j <= 4:
                            qc ^= 1
                        for i in R:
                            self.MM(Wb[:, reg(i)], self.identb[:], Mt4[mc][:, i, :], i == 0, False, ["identb", ("Mt4", mc)], [kW])
                        for i in R:
                            self.MM(Wb[:, reg(i)], Yr4[yc][:, i, :], Mt4[mc][:, i, :], False, True, [("Yr4", yc), ("Mt4", mc)], [kW])
                        self.A(Mt4[mc ^ 1][:], v3(Wb), AF.Copy, [], [("Mt4", mc ^ 1), kW])
                        mc ^= 1
                        yield
                    for i in R:
                        self.MM(Yb[:, reg(i)], Mt4[mc][:, i, :], vbt4[:, i, :], True, True, [("Mt4", mc), "vbt4"], [kY])
                        self.MM(Zb[:, reg(i)], kbg4[:, i, :], Mt4[mc][:, i, :], True, True, [("Mt4", mc), "kbg4"], [kZ])
                    self.A(USB[:, gs, :], v3(Yb), AF.Copy, [], gk("USB") + [kY])
                    self.A(WTn[:, gs, :], v3(Zb), AF.Copy, [], gk("WTn") + [kZ], scale=-1.0)
                    yield

                kb.op(kb.dve, lambda e: e.memset(S[:], 0.0), (), ["S"])
                kb.op(kb.dve, lambda e: e.memset(Sb[0][:], 0.0), (), [("Sb", 0)])
                st = {"si": 0}

                def s2(g0):
                    for pr in range(g0, g0 + G4):
                        cols = slice(pr * 128, (pr + 1) * 128)
                        v = pr % 2
                        sbu = []
                        for c in range(2):
                            si = st["si"]
                            pp = slice(c * 64, (c + 1) * 64)
                            self.MM(pU[pp, 0:128], self.identb[:, pp], USB[:, pr, :], True, False, ["identb", ("USB", pr)], [kU])
                            self.MM(pU[pp, 0:128], WTn[:, pr, pp], Sb[si][:], False, True, [("WTn", pr), ("Sb", si)], [kU])
                            self.A(vn[v][pp, :], pU[pp, 0:128], AF.Copy, [], [("vn", v), kU])
                            sbu.append(si)
                            self.MM(pS[:, 0:128], KDC[pp, pr, :], vn[v][pp, :], True, True, [("KDC", pr), ("vn", v)], [kS])
                            ci = pr * 2 + c
                            self.STT(S[:], S[:], lastc_[:, ci:ci + 1], pS[:, 0:128], ALU.mult, ALU.add, ["S", ("lastc", p)], ["S", kS])
                            si = (si + 1) % 4
                            st["si"] = si
                            self.A(Sb[si][:], S[:], AF.Copy, ["S"], [("Sb", si)])
                            yield
                        self.MM(pU[:, 128:256], vn[v][:], QKT[:, pr, :], True, False, [("vn", v), ("QKT", pr)], [kU])
                        for c in range(2):
                            self.MM(pU[:, 128 + c * 64:128 + (c + 1) * 64], Sb[sbu[c]][:], qdec_[:, pr * 128 + c * 64:pr * 128 + (c + 1) * 64],
                                    False, c == 1, [("Sb", sbu[c]), kQD], [kU])
                        self.A(O[:, cols], pU[:, 128:256], AF.Copy, [], [("O", pr), kU])

                yield from rr_g(s1(0))
                for g in range(4):
                    yield from rr_g(s1((g + 1) * G4) if g < 3 else None, s2(g * G4))
                for tb in range(NTB):
                    ok = [("O", pr) for pr in range(tb * 4, tb * 4 + 4)]
                    self.A(sqt2[:], O[:, tbs(tb)], AF.Square, ok, ["sqt2"])
                    self.MM(Wb[:], self.onesb[:], sqt2[:], True, True, ["onesb", "sqt2"], [kW])
                    self.A(rn2[:], Wb[:], AF.Ln, [], ["rn2", kW], scale=1.0 / 128.0, bias=NORM_EPS)
                    self.A(rn2[:], rn2[:], AF.Exp, ["rn2"], ["rn2"], scale=-0.5)
                    self.STT(O[:, tbs(tb)], O[:, tbs(tb)], self.vcol("gdn_ng", 0), rn2[:], ALU.mult, ALU.mult, ok + ["vec", "rn2"], ok)
                    self.TT(qnb_[:, tbs(tb)], O[:, tbs(tb)], sgd_[:, tbs(tb)], ALU.mult, ok + [("sgd", p, tb), kQ], [("ytc", tb), kQ])
                    yield
                yk = [("ytc", tb) for tb in range(NTB)]
                kb.dma(kb.q_sp, self.yscr[2, h * 128:(h + 1) * 128, :], qnb_[:], reads=yk + [kQ], writes=[("yscr", 2, h)])
                self.dump("yc", qnb_[:], lambda o: o[h * 128:(h + 1) * 128, :], yk + [kQ])

            def run_rr(*gens):
                for _ in rr_g(*gens):
                    pass

            run_rr(S0(0))
            for h in range(NH):
                run_rr(S0(h + 1) if h + 1 < NH else None, S12(h))
            self.barrier()


def build_program(n_seq=SEQ_PER_CORE, layers=tuple(range(DEPTH)), lite=False):
    P = Prog(n_seq, list(layers), lite=lite)
    P.setup()
    for s in range(n_seq):
        P.load_x0(s)
        for l in layers:
            P.layer_setup(l)
            P.barrier()
            bg_es, bg_sb = P.phase_alloc()
            with bg_es:
                P.bg_bank_i = 0
                P.bg_open = False
                P.bg = P.gates_bg(l, bg_sb)
                P.bg_banks = [0, 1]
                P.phase_lru(l)
                P.bg_banks = [3, 4]
                P.phase_gla(l)
                P.bg = None
            P.phase_gdn(l)
            P.phase_merge(l, s)
            P.phase_ffn(l, s, final=(l == layers[-1]))
    P.barrier()
    P.kb.close()
    return P


def kernel(**inputs):
    inp = {k: np.asarray(v) for k, v in inputs.items()}
    W = prep_weights(inp)
    x = inp["x"]
    p = inp["p"]
    B = x.shape[0]
    n_seq = B // N_CORES
    P = build_program(n_seq)
    in_maps = []
    for c in range(N_CORES):
        m = dict(W)
        bs = slice(c * n_seq, (c + 1) * n_seq)
        m["xT"] = np.ascontiguousarray(x[bs].transpose(0, 2, 1))
        m["pT"] = np.ascontiguousarray(p[:, bs].transpose(0, 1, 3, 2))
        in_maps.append(m)
    res = run_bass_kernel_spmd(P.nc, in_maps, core_ids=list(range(N_CORES)))
    out = np.empty((B, T, D), np.float32)
    for c in range(N_CORES):
        o = np.asarray(res.results[c]["outT"])
        out[c * n_seq:(c + 1) * n_seq] = o.transpose(0, 2, 1)
    return out
```
